# Optimizing a Trainium2 kernel written in Bass

```python
import jax, jax.numpy as jnp
from jax import lax
import numpy as np

D_MODEL = 1024
BATCH = 8
SEQ = 2048
DEPTH = 2
DEC_BATCH = 128
DEC_SEQ = 8
PAST_LEN = 16384
PAGE_SIZE = 128

N_EVEN = (DEPTH + 1) // 2
N_ODD = DEPTH // 2
A_WIDTH = D_MODEL // 2
B_WIDTH = D_MODEL // 2
RWKV_HEAD = 64
RWKV_HEADS = B_WIDTH // RWKV_HEAD
LORA_W = 64
LORA_A = 64
LORA_G = 128
B_PROJ = 3 * B_WIDTH + LORA_W + LORA_A + LORA_G
IN_PROJ = 2 * A_WIDTH + B_PROJ
CONV_W = 31
CONV_BUF = CONV_W - 1
POOL_WINDOWS = (2, 4, 8, 16)
POOL_GROUPS = 4
POOL_GW = D_MODEL // POOL_GROUPS
POOL_BUF = max(POOL_WINDOWS) - 1
D_FF = ((8 * D_MODEL + 3 * 256 - 1) // (3 * 256)) * 256
ALPHA = (2 * DEPTH) ** 0.25
BETA = (8 * DEPTH) ** -0.25
LN_EPS = 1e-5
GN_EPS = 64e-5

kernel_name = 'hybrid_conv_rwkv7_pool_deepnorm_step'


def layer_norm(x, g, b, eps=LN_EPS):
    xf = x.astype(jnp.float32)
    mu = jnp.mean(xf, axis=-1, keepdims=True)
    var = jnp.mean(jnp.square(xf - mu), axis=-1, keepdims=True)
    y = (xf - mu) * lax.rsqrt(var + eps) * g.astype(jnp.float32) + b.astype(jnp.float32)
    return y.astype(x.dtype)


def conformer_conv(val, gate, conv_prev, w, b, g, bb):
    u = val * jax.nn.sigmoid(gate)
    ext = jnp.concatenate([conv_prev.astype(u.dtype), u], axis=1)
    y = lax.conv_general_dilated(ext, w[:, None, :].astype(u.dtype), window_strides=(1,), padding='VALID',
                                 dimension_numbers=('NWC', 'WIO', 'NWC'), feature_group_count=A_WIDTH) + b
    y = jax.nn.silu(layer_norm(y, g, bb))
    return y, ext[:, -CONV_BUF:]


def rwkv_scan(r, k, v, w, a, b, s0):
    def step(S, inp):
        r_t, k_t, v_t, w_t, a_t, b_t = inp
        sa = jnp.einsum('nhvk,nhk->nhv', S, a_t)
        S = S * w_t[:, :, None, :] + sa[..., None] * b_t[:, :, None, :] + v_t[..., None] * k_t[:, :, None, :]
        o = jnp.einsum('nhvk,nhk->nhv', S, r_t)
        return S, o
    xs = tuple(jnp.moveaxis(t.astype(jnp.float32), 1, 0) for t in (r, k, v, w, a, b))
    S, o = lax.scan(step, s0.astype(jnp.float32), xs)
    return jnp.moveaxis(o, 0, 1), S


def rwkv_mix(proj, shift_prev, wkv_prev, mu, w0, w2, a0, a2, g2, k_k, k_a, r_k, lnx_g, lnx_b):
    N, L, _ = proj.shape
    prev = jnp.concatenate([shift_prev[:, None, :].astype(proj.dtype), proj[:, :-1]], axis=1)
    xm = proj + mu * (prev - proj)
    s1, s2, s3 = B_WIDTH, 2 * B_WIDTH, 3 * B_WIDTH
    r, k, v = xm[..., :s1], xm[..., s1:s2], xm[..., s2:s3]
    dw = xm[..., s3:s3 + LORA_W]
    da = xm[..., s3 + LORA_W:s3 + LORA_W + LORA_A]
    dg = xm[..., s3 + LORA_W + LORA_A:]
    wlog = -jax.nn.softplus(-(w0 + jnp.tanh(dw) @ w2)) - 0.5
    decay = jnp.exp(-jnp.exp(wlog.astype(jnp.float32)))
    a = jax.nn.sigmoid((a0 + da @ a2).astype(jnp.float32))
    g = jax.nn.sigmoid(dg) @ g2
    hs = (N, L, RWKV_HEADS, RWKV_HEAD)
    rf = r.astype(jnp.float32).reshape(hs)
    kf = k.astype(jnp.float32)
    vf = v.astype(jnp.float32).reshape(hs)
    kk = (kf * k_k.astype(jnp.float32)).reshape(hs)
    kk = kk / jnp.maximum(jnp.sqrt(jnp.sum(kk * kk, axis=-1, keepdims=True)), 1e-12)
    kf = (kf * (1.0 + (a - 1.0) * k_a.astype(jnp.float32))).reshape(hs)
    ah = a.reshape(hs)
    o, S = rwkv_scan(rf, kf, vf, decay.reshape(hs), -kk, kk * ah, wkv_prev)
    om = jnp.mean(o, axis=-1, keepdims=True)
    ov = jnp.mean(jnp.square(o - om), axis=-1, keepdims=True)
    o = ((o - om) * lax.rsqrt(ov + GN_EPS)).reshape(N, L, B_WIDTH) * lnx_g.astype(jnp.float32) + lnx_b.astype(jnp.float32)
    bonus = jnp.sum(rf * kf * r_k.astype(jnp.float32), axis=-1, keepdims=True) * vf
    o = (o + bonus.reshape(N, L, B_WIDTH)) * g.astype(jnp.float32)
    return o.astype(proj.dtype), proj[:, -1], S.astype(wkv_prev.dtype)


def pool_mix(x, pool_prev, start, w, scale):
    N, L, D = x.shape
    ext = jnp.concatenate([pool_prev.astype(x.dtype), x], axis=1)
    cs = jnp.cumsum(ext.astype(jnp.float32), axis=1)
    cs0 = jnp.concatenate([jnp.zeros((N, 1, D), jnp.float32), cs], axis=1)
    pos = start + jnp.arange(L)
    outs = []
    for gi, wdw in enumerate(POOL_WINDOWS):
        lo, hi = gi * POOL_GW, (gi + 1) * POOL_GW
        ssum = cs0[:, POOL_BUF + 1:, lo:hi] - cs0[:, POOL_BUF + 1 - wdw:POOL_BUF + 1 - wdw + L, lo:hi]
        cnt = jnp.minimum(wdw, pos + 1).astype(jnp.float32)
        outs.append(ssum / cnt[None, :, None])
    pooled = jnp.concatenate(outs, axis=-1) - x.astype(jnp.float32)
    y = jnp.einsum('nlgc,gcd->nlgd', pooled.reshape(N, L, POOL_GROUPS, POOL_GW), w.astype(jnp.float32))
    y = y.reshape(N, L, D) * scale.astype(jnp.float32)
    return y.astype(x.dtype), ext[:, -POOL_BUF:]


def swiglu(x, wg, wu, wd):
    return (jax.nn.silu(x @ wg) * (x @ wu)) @ wd


def trunk(x, conv_prev, shift_prev, wkv_prev, pool_prev, start, p):
    new_conv, new_shift, new_wkv, new_pool = [], [], [], []
    for i in range(DEPTH):
        if i % 2 == 0:
            e = i // 2
            proj = x @ p['w_in'][e]
            a_out, c_new = conformer_conv(proj[..., :A_WIDTH], proj[..., A_WIDTH:2 * A_WIDTH], conv_prev[e],
                                          p['conv_w'][e], p['conv_b'][e], p['conv_ln_g'][e], p['conv_ln_b'][e])
            b_out, sh_new, s_new = rwkv_mix(proj[..., 2 * A_WIDTH:], shift_prev[e], wkv_prev[e],
                                            p['rwkv_mu'][e], p['rwkv_w0'][e], p['rwkv_w2'][e], p['rwkv_a0'][e],
                                            p['rwkv_a2'][e], p['rwkv_g2'][e], p['rwkv_kk'][e], p['rwkv_ka'][e],
                                            p['rwkv_rk'][e], p['rwkv_lnx_g'][e], p['rwkv_lnx_b'][e])
            mix = jnp.concatenate([a_out, b_out], axis=-1) @ p['w_out'][e]
            new_conv.append(c_new)
            new_shift.append(sh_new)
            new_wkv.append(s_new)
        else:
            o = i // 2
            mix, pl_new = pool_mix(x, pool_prev[o], start, p['pool_w'][o], p['pool_scale'][o])
            new_pool.append(pl_new)
        x = layer_norm(ALPHA * x + mix, p['ln_mix_g'][i], p['ln_mix_b'][i])
        x = layer_norm(ALPHA * x + swiglu(x, p['ffn_gate'][i], p['ffn_up'][i], p['ffn_down'][i]),
                       p['ln_ffn_g'][i], p['ln_ffn_b'][i])
    return x, jnp.stack(new_conv), jnp.stack(new_shift), jnp.stack(new_wkv), jnp.stack(new_pool)


def setup_inputs(seed: int = 0) -> dict:
    key = jax.random.key(seed)
    ks = jax.random.split(key, 40)
    nrm = jax.random.normal
    f = jnp.float32
    d = {}
    d['x_prompt'] = nrm(ks[0], (BATCH, SEQ, D_MODEL), f)
    d['x_sample'] = nrm(ks[1], (DEC_BATCH, DEC_SEQ, D_MODEL), f)
    d['state_conv'] = 0.5 * nrm(ks[2], (N_EVEN, DEC_BATCH, CONV_BUF, A_WIDTH), f)
    d['state_shift'] = nrm(ks[3], (N_EVEN, DEC_BATCH, B_PROJ), f)
    d['state_wkv'] = 0.5 * nrm(ks[4], (N_EVEN, DEC_BATCH, RWKV_HEADS, RWKV_HEAD, RWKV_HEAD), f)
    d['state_pool'] = nrm(ks[5], (N_ODD, DEC_BATCH, POOL_BUF, D_MODEL), f)
    d['w_in'] = nrm(ks[6], (N_EVEN, D_MODEL, IN_PROJ), f) * D_MODEL ** -0.5
    d['conv_w'] = nrm(ks[7], (N_EVEN, CONV_W, A_WIDTH), f) * CONV_W ** -0.5
    d['conv_b'] = 0.01 * nrm(ks[8], (N_EVEN, A_WIDTH), f)
    d['conv_ln_g'] = 1.0 + 0.02 * nrm(ks[9], (N_EVEN, A_WIDTH), f)
    d['conv_ln_b'] = 0.01 * nrm(ks[10], (N_EVEN, A_WIDTH), f)
    d['rwkv_mu'] = jax.random.uniform(ks[11], (N_EVEN, B_PROJ), f)
    d['rwkv_w0'] = jax.random.uniform(ks[12], (N_EVEN, B_WIDTH), f, -6.0, 1.0)
    d['rwkv_w2'] = 0.1 * nrm(ks[13], (N_EVEN, LORA_W, B_WIDTH), f) * LORA_W ** -0.5
    d['rwkv_a0'] = 0.1 * nrm(ks[14], (N_EVEN, B_WIDTH), f)
    d['rwkv_a2'] = 0.1 * nrm(ks[15], (N_EVEN, LORA_A, B_WIDTH), f) * LORA_A ** -0.5
    d['rwkv_g2'] = nrm(ks[16], (N_EVEN, LORA_G, B_WIDTH), f) * LORA_G ** -0.5
    d['rwkv_kk'] = 0.85 + 0.05 * nrm(ks[17], (N_EVEN, B_WIDTH), f)
    d['rwkv_ka'] = 1.0 + 0.05 * nrm(ks[18], (N_EVEN, B_WIDTH), f)
    d['rwkv_rk'] = 0.1 * nrm(ks[19], (N_EVEN, RWKV_HEADS, RWKV_HEAD), f)
    d['rwkv_lnx_g'] = 1.0 + 0.02 * nrm(ks[20], (N_EVEN, B_WIDTH), f)
    d['rwkv_lnx_b'] = 0.01 * nrm(ks[21], (N_EVEN, B_WIDTH), f)
    d['w_out'] = nrm(ks[22], (N_EVEN, A_WIDTH + B_WIDTH, D_MODEL), f) * (A_WIDTH + B_WIDTH) ** -0.5 * BETA
    d['pool_w'] = nrm(ks[23], (N_ODD, POOL_GROUPS, POOL_GW, POOL_GW), f) * POOL_GW ** -0.5 * BETA
    d['pool_scale'] = 1.0 + 0.02 * nrm(ks[24], (N_ODD, D_MODEL), f)
    d['ln_mix_g'] = 1.0 + 0.02 * nrm(ks[25], (DEPTH, D_MODEL), f)
    d['ln_mix_b'] = 0.01 * nrm(ks[26], (DEPTH, D_MODEL), f)
    d['ffn_gate'] = nrm(ks[27], (DEPTH, D_MODEL, D_FF), f) * D_MODEL ** -0.5
    d['ffn_up'] = nrm(ks[28], (DEPTH, D_MODEL, D_FF), f) * D_MODEL ** -0.5
    d['ffn_down'] = nrm(ks[29], (DEPTH, D_FF, D_MODEL), f) * D_FF ** -0.5 * BETA
    d['ln_ffn_g'] = 1.0 + 0.02 * nrm(ks[30], (DEPTH, D_MODEL), f)
    d['ln_ffn_b'] = 0.01 * nrm(ks[31], (DEPTH, D_MODEL), f)
    return d


def reference(x_prompt, x_sample, state_conv, state_shift, state_wkv, state_pool,
              w_in, conv_w, conv_b, conv_ln_g, conv_ln_b,
              rwkv_mu, rwkv_w0, rwkv_w2, rwkv_a0, rwkv_a2, rwkv_g2, rwkv_kk, rwkv_ka, rwkv_rk,
              rwkv_lnx_g, rwkv_lnx_b, w_out, pool_w, pool_scale,
              ln_mix_g, ln_mix_b, ffn_gate, ffn_up, ffn_down, ln_ffn_g, ln_ffn_b):
    p = dict(w_in=w_in, conv_w=conv_w, conv_b=conv_b, conv_ln_g=conv_ln_g, conv_ln_b=conv_ln_b,
             rwkv_mu=rwkv_mu, rwkv_w0=rwkv_w0, rwkv_w2=rwkv_w2, rwkv_a0=rwkv_a0, rwkv_a2=rwkv_a2,
             rwkv_g2=rwkv_g2, rwkv_kk=rwkv_kk, rwkv_ka=rwkv_ka, rwkv_rk=rwkv_rk,
             rwkv_lnx_g=rwkv_lnx_g, rwkv_lnx_b=rwkv_lnx_b, w_out=w_out, pool_w=pool_w, pool_scale=pool_scale,
             ln_mix_g=ln_mix_g, ln_mix_b=ln_mix_b, ffn_gate=ffn_gate, ffn_up=ffn_up, ffn_down=ffn_down,
             ln_ffn_g=ln_ffn_g, ln_ffn_b=ln_ffn_b)
    dt = x_prompt.dtype
    z_conv = jnp.zeros((N_EVEN, BATCH, CONV_BUF, A_WIDTH), dt)
    z_shift = jnp.zeros((N_EVEN, BATCH, B_PROJ), dt)
    z_wkv = jnp.zeros((N_EVEN, BATCH, RWKV_HEADS, RWKV_HEAD, RWKV_HEAD), state_wkv.dtype)
    z_pool = jnp.zeros((N_ODD, BATCH, POOL_BUF, D_MODEL), dt)
    y_prompt, p_conv, p_shift, p_wkv, p_pool = trunk(x_prompt, z_conv, z_shift, z_wkv, z_pool, 0, p)
    y_sample, s_conv, s_shift, s_wkv, s_pool = trunk(x_sample, state_conv, state_shift, state_wkv, state_pool,
                                                     PAST_LEN, p)
    return (y_prompt, y_sample, p_conv, p_shift, p_wkv, p_pool, s_conv, s_shift, s_wkv, s_pool)
```

```python
from contextlib import ExitStack
import numpy as np
import concourse.bass as bass
import concourse.mybir as mybir
from concourse.bass_utils import run_bass_kernel_spmd

F32 = mybir.dt.float32
BF16 = mybir.dt.bfloat16
AF = mybir.ActivationFunctionType
ALU = mybir.AluOpType

NCORES = 8
D = 1024
NT = 17
INP = 2816
DFF = 2816
ALPHA = 4 ** 0.25
LN_EPS = 1e-5
GN_EPS = 64e-5
NLEV = 6
DECAY_C = -float(np.exp(-0.5))


import os
STOP = int(os.environ.get("MK_STOP", "99"))


class StopBuild(Exception):
    pass


def checkpoint(n):
    if STOP <= n:
        raise StopBuild()


class Buf:
    __slots__ = ("name", "w", "r", "wx")

    def __init__(self, name):
        self.name = name
        self.w = None
        self.r = {}
        self.wx = {}


class Multi:
    def __init__(self, members):
        self.members = list(members)


def flat(bufs):
    out = []
    for b in bufs:
        if isinstance(b, Multi):
            out.extend(b.members)
        else:
            out.append(b)
    return out


def run(g):
    for _ in g:
        pass


def merge(a, b):
    da = db = False
    while not (da and db):
        if not da:
            try:
                next(a)
            except StopIteration:
                da = True
        if not db:
            try:
                next(b)
            except StopIteration:
                db = True
        yield


class Sched:
    CE = ("pe", "act", "dve", "pool")

    def __init__(self, nc, ctx):
        self.nc = nc
        self.eng = {"pe": nc.tensor, "act": nc.scalar, "dve": nc.vector, "pool": nc.gpsimd, "sp": nc.sync}
        self.sem = {e: ctx.enter_context(nc.semaphore("s_" + e)) for e in self.CE}
        self.cnt = {e: 0 for e in self.CE}
        self.ndq = 6
        self.dq = {}
        for q in ("sp", "pool"):
            for i in range(self.ndq):
                nm = "d%s%d" % (q, i)
                self.sem[nm] = ctx.enter_context(nc.semaphore(nm))
                self.cnt[nm] = 0
            self.dq[q] = 0
        self.seen = {e: {} for e in ("pe", "act", "dve", "pool", "sp")}
        self.snap = {}
        self.nwait = 0
        self.nins = 0

    def _wait(self, e, src, c):
        if c <= 0 or self.seen[e].get(src, 0) >= c:
            return
        self.eng[e].wait_ge(self.sem[src], c)
        self.nwait += 1
        self._absorb(e, src, c)

    def _absorb(self, e, src, c):
        se = self.seen[e]
        if se.get(src, 0) < c:
            se[src] = c
        sn = self.snap.get((src, c))
        if sn:
            for k, v in sn.items():
                if se.get(k, 0) < v:
                    se[k] = v

    def _deps(self, reads, writes):
        deps = {}
        for b in reads:
            if b.w is not None:
                s, c = b.w
                deps[s] = max(deps.get(s, 0), c)
            for s, c in b.wx.items():
                deps[s] = max(deps.get(s, 0), c)
        for b in writes:
            if b.w is not None:
                s, c = b.w
                deps[s] = max(deps.get(s, 0), c)
            for s, c in b.r.items():
                deps[s] = max(deps.get(s, 0), c)
        return deps

    def _mark(self, src, c, reads, writes):
        for b in reads:
            b.r[src] = c
        for b in writes:
            if src[0] == "d" and src not in self.CE:
                if b.w is not None and b.w[0] != src and b.w[0] not in self.CE:
                    b.wx[b.w[0]] = max(b.wx.get(b.w[0], 0), b.w[1])
                b.wx.pop(src, None)
            else:
                b.wx = {}
            b.w = (src, c)
            b.r = {}

    def op(self, e, fn, reads=(), writes=()):
        reads, writes = flat(reads), flat(writes)
        deps = self._deps(reads, writes)
        for s, c in deps.items():
            if s == e and e == "pe":
                continue
            self._wait(e, s, c)
        ins = fn()
        self.cnt[e] += 1
        ins.then_inc(self.sem[e], 1)
        self.snap[(e, self.cnt[e])] = dict(self.seen[e])
        self._mark(e, self.cnt[e], reads, writes)
        self.nins += 1
        return ins

    def dma(self, q, out, in_, reads=(), writes=(), parallel=False):
        reads, writes = flat(reads), flat(writes)
        deps = self._deps(reads, writes)
        if parallel:
            for b in writes:
                if b.w is not None and b.w[0] not in self.CE:
                    deps = self._deps(reads, [])
                    for b2 in writes:
                        if b2.w is not None and b2.w[0] in self.CE:
                            deps[b2.w[0]] = max(deps.get(b2.w[0], 0), b2.w[1])
                        for s_, c_ in b2.r.items():
                            deps[s_] = max(deps.get(s_, 0), c_)
                    break
        i = self.dq[q]
        self.dq[q] = (i + 1) % self.ndq
        nm = "d%s%d" % (q, i)
        deps[nm] = max(deps.get(nm, 0), self.cnt[nm])
        for s, c in deps.items():
            self._wait(q, s, c)
        ins = self.eng[q].dma_start(out=out, in_=in_)
        self.cnt[nm] += 16
        ins.then_inc(self.sem[nm], 16)
        self.snap[(nm, self.cnt[nm])] = dict(self.seen[q])
        self._mark(nm, self.cnt[nm], reads, writes)
        self.nins += 1
        return ins

    def barrier(self):
        for e in ("pe", "act", "dve", "pool", "sp"):
            for s, c in self.cnt.items():
                if s != e:
                    self._wait(e, s, c)


def make_consts():
    c = {}
    c["c_ident"] = np.eye(128, dtype=np.float32)
    blk = np.arange(128) // 64
    c["c_bones"] = (blk[:, None] == blk[None, :]).astype(np.float32)
    c["c_ones"] = np.ones((128, 128), np.float32)
    i = np.arange(128)
    for nm, bs in (("p", 128), ("s", 8)):
        same = (i[:, None] // bs) == (i[None, :] // bs)
        strict = ((i[None, :] > i[:, None]) & same).astype(np.float32)
        incl = ((i[None, :] >= i[:, None]) & same).astype(np.float32)
        c["c_mu_" + nm] = np.concatenate([strict, incl], axis=1)
        c["c_ml_" + nm] = ((i[None, :] < i[:, None]) & same).astype(np.float32)
    rm_p = np.ones((128, 512), np.float32)
    rm_p[:, ::128] = 0
    rm_s = np.ones((128, 128), np.float32)
    rm_s[:, ::8] = 0
    c["c_rm_p"] = rm_p
    c["c_rm_s"] = rm_s
    cm = np.zeros((16, 128), np.float32)
    for s in range(16):
        cm[s, s * 8:(s + 1) * 8] = 1
    c["c_colmask"] = np.broadcast_to(cm[None], (128, 16, 128)).copy()
    c["c_rowmask"] = cm.T.copy()
    hm = np.zeros((128, 2), np.float32)
    hm[:64, 0] = 1
    hm[64:, 1] = 1
    c["c_headmask"] = hm
    band = np.zeros((128, 4, 6, 128), np.float32)
    for g, w in enumerate((2, 4, 8, 16)):
        for t in range(128):
            for s in range(128):
                d = t - s
                if 0 <= d < w:
                    band[s, g, 0, t] += 1.0 / w
                    band[s, g, 2, t] += 1.0 / min(w, t + 1)
                    if s // 8 == t // 8:
                        band[s, g, 3, t] += 1.0 / w
                if s == t:
                    band[s, g, 0, t] -= 1.0
                    band[s, g, 2, t] -= 1.0
                    band[s, g, 3, t] -= 1.0
                if 0 <= t + 128 - s < w:
                    band[s, g, 1, t] += 1.0 / w
            seq, tl = t // 8, t % 8
            half, q = seq // 8, seq % 8
            for j in range(15):
                if j >= 15 + tl - w + 1:
                    band[q * 15 + j, g, 4 + half, t] += 1.0 / w
    c["c_band"] = band
    return c


def build():
    nc = bass.Bass("TRN2", target_bir_lowering=False)
    consts = make_consts()

    def din(name, shape):
        return nc.dram_tensor(name, list(shape), F32, kind="ExternalInput").ap()

    def dout(name, shape):
        return nc.dram_tensor(name, list(shape), F32, kind="ExternalOutput").ap()

    xp = din("xp", (2048, D))
    xs = din("xs", (128, D))
    sconv = din("sconv", (16, 30, 512))
    sshift = din("sshift", (16, 1792))
    swkv = din("swkv", (16, 8, 64, 64))
    spool = din("spool", (16, 15, D))
    w_in = din("w_in", (D, INP))
    conv_w = din("conv_w", (31, 512))
    vecs = din("vecs", (10, 512))
    mu = din("mu", (14, 128))
    w2a2 = din("w2a2", (128, 512))
    g2 = din("g2", (128, 512))
    w_out = din("w_out", (D, D))
    pool_w = din("pool_w", (4, 256, 256))
    pool_scale = din("pool_scale", (1, D))
    ln_mix_g = din("ln_mix_g", (2, D))
    ln_mix_b = din("ln_mix_b", (2, D))
    ffn_gate = din("ffn_gate", (2, D, DFF))
    ffn_up = din("ffn_up", (2, D, DFF))
    ffn_down = din("ffn_down", (2, DFF, D))
    ln_ffn_g = din("ln_ffn_g", (2, D))
    ln_ffn_b = din("ln_ffn_b", (2, D))
    cd = {k: din(k, v.shape) for k, v in consts.items()}

    y_p = dout("y_p", (2048, D))
    y_s = dout("y_s", (128, D))
    o_conv_p = dout("o_conv_p", (30, 512))
    o_shift_p = dout("o_shift_p", (1, 1792))
    o_wkv_p = dout("o_wkv_p", (8, 64, 64))
    o_pool_p = dout("o_pool_p", (15, D))
    o_conv_s = dout("o_conv_s", (16, 30, 512))
    o_shift_s = dout("o_shift_s", (16, 1792))
    o_wkv_s = dout("o_wkv_s", (16, 8, 64, 64))
    o_pool_s = dout("o_pool_s", (16, 15, D))
    outbufs = []

    with ExitStack() as ctx:
        S = Sched(nc, ctx)

        _names = {}

        def sbt(st, name, shape, dt):
            k = _names.get(name, 0)
            _names[name] = k + 1
            if k:
                name = "%s_%d" % (name, k)
            return st.enter_context(nc.sbuf_tensor(name, list(shape), dt))

        def V(fn, r=(), w=()):
            return S.op("dve", fn, r, w)

        def A(fn, r=(), w=()):
            return S.op("act", fn, r, w)

        def G(fn, r=(), w=()):
            return S.op("pool", fn, r, w)

        def T(fn, r=(), w=()):
            return S.op("pe", fn, r, w)

        def odma(out, in_, reads):
            b = Buf("out")
            S.dma("sp", out, in_, reads=reads, writes=[b])
            outbufs.append(b)

        PS = ctx.enter_context(nc.psum_tensor("PS", [128, 8, 512], F32))
        _pbk = [Buf("ps%d" % i) for i in range(8)]
        ph = [[_pbk[i], _pbk[i]] for i in range(8)]
        pb = [Multi([_pbk[i]]) for i in range(8)]
        x_all = sbt(ctx, "x_all", (128, NT, D), F32)
        bx = [Buf("x%d" % i) for i in range(NT)]
        ident = sbt(ctx, "ident", (128, 128), F32)
        identb = sbt(ctx, "identb", (128, 128), BF16)
        bonesf = sbt(ctx, "bonesf", (128, 128), F32)
        bonesb = sbt(ctx, "bonesb", (128, 128), BF16)
        lng = sbt(ctx, "lng", (128, D), F32)
        lnb = sbt(ctx, "lnb", (128, D), F32)
        lntmp = sbt(ctx, "lntmp", (128, D), F32)
        lnst = sbt(ctx, "lnst", (128, 2, 6), F32)
        lnmv = sbt(ctx, "lnmv", (128, 4), F32)
        b_c = Buf("consts")
        b_ln = Buf("lnconst")
        b_lnq = [Buf("lntmp%d" % i) for i in range(4)]
        b_lntmp = Multi(b_lnq)
        b_lnst = Buf("lnst")

        S.dma("sp", ident[:], cd["c_ident"], writes=[b_c])
        S.dma("sp", bonesf[:], cd["c_bones"], writes=[b_c])
        V(lambda: nc.vector.tensor_copy(identb[:], ident[:]), [b_c], [b_c])
        V(lambda: nc.vector.tensor_copy(bonesb[:], bonesf[:]), [b_c], [b_c])
        def load_x():
            for q in range(4):
                S.dma("sp", x_all[:, 4 * q:4 * q + 4, :],
                      xp[512 * q:512 * (q + 1), :].rearrange("(t p) d -> p t d", p=128), writes=bx[4 * q:4 * q + 4])
            S.dma("sp", x_all[:, 16, :], xs, writes=[bx[16]])

        def bcast_row(dst, row_ap, dst_bufs, banks=(2, 3)):
            S.dma("sp", lntmp[0:1, :], row_ap, writes=[b_lntmp])
            for hf in range(2):
                for ph_ in range(2):
                    T(lambda: nc.tensor.matmul(PS[ph_ * 64:(ph_ + 1) * 64, banks[hf], :], bonesf[0:1, 0:64], lntmp[0:1, hf * 512:(hf + 1) * 512],
                                               start=True, stop=True), [b_c, b_lntmp], [pb[banks[hf]]])
                if hf == 0:
                    V(lambda: nc.vector.tensor_copy(dst[:, 0:512], PS[:, banks[0], :]), [pb[banks[0]]], dst_bufs)
                else:
                    A(lambda: nc.scalar.copy(dst[:, 512:1024], PS[:, banks[1], :]), [pb[banks[1]]], dst_bufs)

        def load_ln(gsrc, bsrc, i):
            bcast_row(lng, gsrc[i:i + 1, :], [b_ln])
            bcast_row(lnb, bsrc[i:i + 1, :], [b_ln])

        def layer_norm_tile(ti, mix_ap, mix_bufs, final_dst=None, gen=False):
            g = _ln_gen(ti, mix_ap, mix_bufs, final_dst)
            if gen:
                return g
            run(g)

        def _ln_gen(ti, mix_ap, mix_bufs, final_dst):
            xt = x_all[:, ti, :]
            V(lambda: nc.vector.scalar_tensor_tensor(lntmp[:], xt, ALPHA, mix_ap, ALU.mult, ALU.add),
              [bx[ti]] + mix_bufs, [b_lntmp])
            V(lambda: nc.vector.bn_stats(lnst[:, 0, :], lntmp[:, 0:512]), [b_lntmp], [b_lnst])
            V(lambda: nc.vector.bn_stats(lnst[:, 1, :], lntmp[:, 512:1024]), [b_lntmp], [b_lnst])
            V(lambda: nc.vector.bn_aggr(lnmv[:, 0:2], lnst[:]), [b_lnst], [b_lnst])
            yield
            V(lambda: nc.vector.tensor_scalar(lnmv[:, 2:3], lnmv[:, 1:2], LN_EPS, None, ALU.add), [b_lnst], [b_lnst])
            G(lambda: nc.gpsimd.tensor_tensor(lnmv[:, 3:4], lnmv[:, 2:3], epsc[:, 2:3], ALU.pow), [b_lnst, b_c], [b_lnst])
            V(lambda: nc.vector.scalar_tensor_tensor(lnmv[:, 2:3], lnmv[:, 0:1], -1.0, lnmv[:, 3:4], ALU.mult, ALU.mult), [b_lnst], [b_lnst])
            A(lambda: nc.scalar.activation(lntmp[:], lntmp[:], AF.Identity, scale=lnmv[:, 3:4], bias=lnmv[:, 2:3]), [b_lntmp, b_lnst], [b_lntmp])
            yield
            V(lambda: nc.vector.tensor_tensor(lntmp[:], lntmp[:], lng[:], ALU.mult), [b_lntmp, b_ln], [b_lntmp])
            V(lambda: nc.vector.tensor_tensor(xt, lntmp[:], lnb[:], ALU.add), [b_lntmp, b_ln], [bx[ti]])
            if final_dst is not None:
                odma(final_dst, xt, [bx[ti]])
            yield

        epsc = sbt(ctx, "epsc", (128, 4), F32)
        V(lambda: nc.vector.memset(epsc[:, 0:1], LN_EPS), [], [b_c])
        V(lambda: nc.vector.memset(epsc[:, 1:2], GN_EPS), [], [b_c])
        V(lambda: nc.vector.memset(epsc[:, 2:3], -0.5), [], [b_c])

        def make_xT(xT, b_xT, tiles, ps_banks=(0, 1), col0=0):
            for j, ti in enumerate(tiles):
                for hf in range(2):
                    bk = ps_banks[hf]
                    for q in range(4):
                        kc = hf * 4 + q
                        T(lambda: nc.tensor.transpose(PS[:, bk, q * 128:(q + 1) * 128], x_all[:, ti, kc * 128:(kc + 1) * 128], ident[:]),
                          [bx[ti], b_c], [pb[bk]])
                    dst = xT[:, hf * 4:hf * 4 + 4, col0 + j * 128:col0 + (j + 1) * 128]
                    src = PS[:, bk, :].rearrange("p (q t) -> p q t", q=4)
                    if hf == 0:
                        V(lambda: nc.vector.tensor_copy(dst, src), [pb[bk]], [b_xT])
                    else:
                        A(lambda: nc.scalar.copy(dst, src), [pb[bk]], [b_xT])

        GROUPS = [([0, 1, 2, 3], 512), ([4, 5, 6, 7], 512), ([8, 9, 10, 11], 512), ([12, 13, 14, 15], 512), ([16], 128)]

        stopped = []
        try:
            with ExitStack() as l0:
                aT_all = sbt(l0, "aT_all", (128, 4, NT * 128), BF16)
                b_aT = Buf("aT")
                pv = sbt(l0, "pv", (128, 4, 10), F32)
                pmu = sbt(l0, "pmu", (128, 14), F32)
                omka = sbt(l0, "omka", (128, 4), F32)
                b_pv = Buf("pv")
                CB, CLG, CLB, W0, A0, KK, KA, RK, LXG, LXB = range(10)
                checkpoint(1)

                with ExitStack() as pa:
                    winA = sbt(pa, "winA", (128, 8, 1024), BF16)
                    onesf = sbt(pa, "onesf", (128, 128), F32)
                    S.dma("sp", onesf[:], cd["c_ones"], writes=[b_c])
                    b_winA = Buf("winA")
                    for kc in range(8):
                        S.dma("pool", winA[:, kc, :], w_in[kc * 128:(kc + 1) * 128, 0:1024], writes=[b_winA], parallel=True)
                    cw = sbt(pa, "cw", (128, 4, 31), F32)
                    diag = sbt(pa, "diag", (128, 4, 31, 128), BF16)
                    b_diag = Buf("diag")
                    cwst = sbt(pa, "cwst", (32, 512), F32)
                    S.dma("sp", cwst[0:31, :], conv_w, writes=[b_diag])
                    vst = sbt(pa, "vst", (16, 512), F32)
                    mst = sbt(pa, "mst", (16, 128), F32)
                    b_vst = Buf("vst")
                    S.dma("sp", vst[0:10, :], vecs, writes=[b_vst])
                    S.dma("sp", mst[0:14, :], mu, writes=[b_vst])
                    load_x()
                    for cc in range(4):
                        T(lambda: nc.tensor.transpose(PS[:, 1, cc * 10:cc * 10 + 10], vst[0:10, cc * 128:(cc + 1) * 128], ident[0:10, 0:10]),
                          [b_vst, b_c], [pb[1]])
                    T(lambda: nc.tensor.transpose(PS[:, 1, 64:78], mst[0:14, :], ident[0:14, 0:14]), [b_vst, b_c], [pb[1]])
                    V(lambda: nc.vector.tensor_copy(pv[:], PS[:, 1, 0:40].rearrange("p (c j) -> p c j", c=4)), [pb[1]], [b_pv])
                    V(lambda: nc.vector.tensor_copy(pmu[:], PS[:, 1, 64:78]), [pb[1]], [b_pv])
                    V(lambda: nc.vector.tensor_scalar(omka[:], pv[:, :, 6], -1.0, 1.0, ALU.mult, ALU.add), [b_pv], [b_pv])
                    for cc in range(4):
                        T(lambda: nc.tensor.transpose(PS[:, 0, cc * 31:cc * 31 + 31], cwst[0:31, cc * 128:(cc + 1) * 128], ident[0:31, 0:31]),
                          [b_diag, b_c], [pb[0]])
                    V(lambda: nc.vector.tensor_copy(cw[:], PS[:, 0, 0:124].rearrange("p (c j) -> p c j", c=4)), [pb[0]], [b_diag])
                    for cc in range(4):
                        V(lambda: nc.vector.tensor_tensor(diag[:, cc, :, :], ident[:].unsqueeze(1).to_broadcast([128, 31, 128]),
                                                          cw[:, cc, :].unsqueeze(2).to_broadcast([128, 31, 128]), ALU.mult),
                          [b_diag, b_c], [b_diag])
                    xT = sbt(pa, "xT_a", (128, 8, 512), BF16)
                    b_xT = Buf("xT")
                    uext = sbt(pa, "uext", (128, 4, 608), F32)
                    uxb = sbt(pa, "uxb", (128, 4, 608), BF16)
                    b_uext = [Buf("uext%d" % i) for i in range(4)]
                    b_uxb = [Buf("uxb%d" % i) for i in range(4)]
                    sgt = sbt(pa, "sgt", (128, 512), F32)
                    b_sgt = Buf("sgt")
                    cacc = sbt(pa, "cacc", (128, 4, 512), F32)
                    csq = sbt(pa, "csq", (128, 4, 512), F32)
                    b_cacc = [Buf("cacc%d" % i) for i in range(4)]
                    b_csq = [Buf("csq%d" % i) for i in range(4)]
                    st_m = sbt(pa, "st_m", (128, 512), F32)
                    st_r = sbt(pa, "st_r", (128, 512), F32)
                    st_t = sbt(pa, "st_t", (128, 512), F32)
                    b_st = Buf("st")
                    b_stt = Buf("stt")
                    ctm = sbt(pa, "ctm", (128, 4, 128), F32)
                    b_ctm = Buf("ctm")
                    for cc in range(4):
                        V(lambda: nc.vector.memset(uext[:, cc, 0:30], 0.0), [], [b_uext[cc]])

                    for gi, (tiles, ntok) in enumerate(GROUPS):
                        sample = (gi == 4)
                        t0 = tiles[0] * 128
                        make_xT(xT, b_xT, tiles)
                        if sample:
                            cst = sbt(pa, "cst", (128, 4, 512), F32)
                            b_cst = Buf("cst")
                            S.dma("sp", cst[0:120, :, :], sconv.rearrange("(a s) r c -> (s r) a c", a=4), writes=[b_cst])
                            for cc in range(4):
                                for a in range(4):
                                    T(lambda: nc.tensor.transpose(PS[:, 4 + cc, a * 120:(a + 1) * 120], cst[0:120, a, cc * 128:(cc + 1) * 128], ident[0:120, 0:120]),
                                      [b_cst, b_c], [pb[4 + cc]])
                                V(lambda: nc.vector.tensor_copy(
                                    uext[:, cc, 0:608].rearrange("p (q r) -> p q r", q=16)[:, :, 0:30],
                                    PS[:, 4 + cc, 0:480].rearrange("p (q r) -> p q r", q=16)), [pb[4 + cc]], [b_uext[cc]])
                        for cc in range(4):
                            for (bk, col0) in ((2, cc * 128), (3, 512 + cc * 128)):
                                for kc in range(8):
                                    T(lambda: nc.tensor.matmul(PS[:, bk, 0:ntok], winA[:, kc, col0:col0 + 128], xT[:, kc, 0:ntok],
                                                               start=(kc == 0), stop=(kc == 7)), [b_winA, b_xT], [pb[bk]])
                            A(lambda: nc.scalar.activation(sgt[:, 0:ntok], PS[:, 3, 0:ntok], AF.Sigmoid), [pb[3]], [b_sgt])
                            if not sample:
                                ud = uext[:, cc, 30:30 + ntok]
                                V(lambda: nc.vector.tensor_tensor(ud, PS[:, 2, 0:ntok], sgt[:, 0:ntok], ALU.mult), [pb[2], b_sgt], [b_uext[cc]])
                                ulen = 30 + ntok
                            else:
                                ud = uext[:, cc, 0:608].rearrange("p (q r) -> p q r", q=16)[:, :, 30:38]
                                V(lambda: nc.vector.tensor_tensor(ud, PS[:, 2, 0:128].rearrange("p (q r) -> p q r", q=16),
                                                                  sgt[:, 0:128].rearrange("p (q r) -> p q r", q=16), ALU.mult),
                                  [pb[2], b_sgt], [b_uext[cc]])
                                ulen = 608
                            A(lambda: nc.scalar.copy(uxb[:, cc, 0:ulen], uext[:, cc, 0:ulen]), [b_uext[cc]], [b_uxb[cc]])
                            for j in range(31):
                                if not sample:
                                    rhs = uxb[:, cc, j:j + ntok]
                                else:
                                    rhs = uxb[:, cc, 0:608].rearrange("p (q r) -> p q r", q=16)[:, :, j:j + 8]
                                T(lambda: nc.tensor.matmul(PS[:, 4, 0:ntok], diag[:, cc, j, :], rhs, start=(j == 0), stop=(j == 30)),
                                  [b_diag, b_uxb[cc]], [pb[4]])
                            A(lambda: nc.scalar.activation(cacc[:, cc, 0:ntok], PS[:, 4, 0:ntok], AF.Identity, bias=pv[:, cc, CB:CB + 1]),
                              [pb[4], b_pv], [b_cacc[cc]])
                            V(lambda: nc.vector.tensor_tensor(csq[:, cc, 0:ntok], cacc[:, cc, 0:ntok], cacc[:, cc, 0:ntok], ALU.mult),
                              [b_cacc[cc]], [b_csq[cc]])
                        for cc in range(4):
                            T(lambda: nc.tensor.matmul(PS[:, 5, 0:ntok], onesf[:], cacc[:, cc, 0:ntok], start=(cc == 0), stop=(cc == 3)),
                              [b_c, b_cacc[cc]], [pb[5]])
                        for cc in range(4):
                            T(lambda: nc.tensor.matmul(PS[:, 6, 0:ntok], onesf[:], csq[:, cc, 0:ntok], start=(cc == 0), stop=(cc == 3)),
                              [b_c, b_csq[cc]], [pb[6]])
                        A(lambda: nc.scalar.mul(st_m[:, 0:ntok], PS[:, 5, 0:ntok], 1.0 / 512), [pb[5]], [b_st])
                        V(lambda: nc.vector.tensor_tensor(st_t[:, 0:ntok], st_m[:, 0:ntok], st_m[:, 0:ntok], ALU.mult), [b_st], [b_stt])
                        V(lambda: nc.vector.scalar_tensor_tensor(st_t[:, 0:ntok], PS[:, 6, 0:ntok], 1.0 / 512, st_t[:, 0:ntok], ALU.mult, ALU.subtract),
                          [pb[6], b_stt], [b_stt])
                        A(lambda: nc.scalar.activation(st_t[:, 0:ntok], st_t[:, 0:ntok], AF.Ln, bias=epsc[:, 0:1]), [b_stt, b_c], [b_stt])
                        A(lambda: nc.scalar.activation(st_r[:, 0:ntok], st_t[:, 0:ntok], AF.Exp, scale=-0.5), [b_stt], [b_st])
                        for cc in range(4):
                            V(lambda: nc.vector.tensor_tensor(csq[:, cc, 0:ntok], cacc[:, cc, 0:ntok], st_m[:, 0:ntok], ALU.subtract),
                              [b_cacc[cc], b_st], [b_csq[cc]])
                            V(lambda: nc.vector.tensor_tensor(csq[:, cc, 0:ntok], csq[:, cc, 0:ntok], st_r[:, 0:ntok], ALU.mult),
                              [b_csq[cc], b_st], [b_csq[cc]])
                            A(lambda: nc.scalar.activation(aT_all[:, cc, t0:t0 + ntok], csq[:, cc, 0:ntok], AF.Silu,
                                                           bias=pv[:, cc, CLB:CLB + 1], scale=pv[:, cc, CLG:CLG + 1]),
                              [b_csq[cc], b_pv], [b_aT])
                        if not sample:
                            if gi < 3:
                                for cc in range(4):
                                    V(lambda: nc.vector.tensor_copy(uext[:, cc, 0:30], uext[:, cc, ntok:ntok + 30]), [b_uext[cc]], [b_uext[cc]])
                            else:
                                for cc in range(4):
                                    T(lambda: nc.tensor.transpose(PS[0:30, 7, cc * 128:(cc + 1) * 128], uext[:, cc, ntok:ntok + 30], ident[:]),
                                      [b_uext[cc], b_c], [pb[7]])
                                V(lambda: nc.vector.tensor_copy(ctm[0:30, :, :], PS[0:30, 7, :].rearrange("p (c j) -> p c j", c=4)), [pb[7]], [b_ctm])
                                odma(o_conv_p, ctm[0:30, :, :].rearrange("p c j -> p (c j)"), [b_ctm])
                                S.barrier()
                        else:
                            for cc in range(4):
                                V(lambda: nc.vector.tensor_copy(ctm[:, cc, :].rearrange("p (q r) -> p q r", q=16),
                                                                uext[:, cc, 0:608].rearrange("p (q r) -> p q r", q=16)[:, :, 30:38]),
                                  [b_uext[cc]], [b_ctm])
                            for cc in range(4):
                                T(lambda: nc.tensor.transpose(PS[:, 7, cc * 128:(cc + 1) * 128], ctm[:, cc, :], ident[:]), [b_ctm, b_c], [pb[7]])
                            V(lambda: nc.vector.tensor_copy(sgt[:, :], PS[:, 7, :]), [pb[7]], [b_sgt])
                            odma(o_conv_s[:, 22:30, :], sgt[:, :], [b_sgt])
                            odma(o_conv_s[:, 0:22, :], sconv[:, 8:30, :], [])
                    S.barrier()

                checkpoint(2)
                with ExitStack() as pbk:
                    winB = sbt(pbk, "winB", (128, 8, 1792), BF16)
                    woutb = sbt(pbk, "woutb", (128, 8, 1024), BF16)
                    lw2 = sbt(pbk, "lw2", (128, 512), BF16)
                    lg2 = sbt(pbk, "lg2", (128, 512), BF16)
                    b_wB = Buf("wB")
                    b_wBc = [Buf("wBc%d" % i) for i in range(14)]

                    def load_wchunk(ci):
                        S.dma("pool", winB[:, :, ci * 128:(ci + 1) * 128],
                              w_in[:, 1024 + ci * 128:1024 + (ci + 1) * 128].rearrange("(k p) n -> p k n", p=128), writes=[b_wBc[ci]])

                    load_wchunk(12)
                    load_wchunk(13)
                    S.dma("pool", lw2[:], w2a2, writes=[b_wB])
                    S.dma("pool", lg2[:], g2, writes=[b_wB])
                    for cc_ in range(4):
                        for base_ in (0, 4, 8):
                            load_wchunk(base_ + cc_)
                    for kc in range(8):
                        S.dma("pool", woutb[:, kc, :], w_out[kc * 128:(kc + 1) * 128, :], writes=[b_wB], parallel=True)
                    load_ln(ln_mix_g, ln_mix_b, 0)
                    mu_t = {}
                    msk = {}
                    rm_s = sbt(pbk, "rm_s", (128, 128), F32)
                    headmask = sbt(pbk, "headmask", (128, 2), F32)
                    S.dma("sp", rm_s[:], cd["c_rm_s"], writes=[b_c])
                    S.dma("sp", headmask[:], cd["c_headmask"], writes=[b_c])

                    b_H = [Buf("H%d" % i) for i in range(4)]
                    b_Hs = [Buf("Hs%d" % i) for i in range(4)]
                    carry = sbt(pbk, "carry", (128, 14), F32)
                    lend = sbt(pbk, "lend", (128, 2), F32)
                    b_lend = Buf("lend")
                    omm = sbt(pbk, "omm", (128, 14), F32)
                    V(lambda: nc.vector.tensor_scalar(omm[:], pmu[:], -1.0, 1.0, ALU.mult, ALU.add), [b_pv], [b_pv])
                    b_cm = Buf("masks")
                    b_carry = [Buf("carry%d" % i) for i in range(14)]
                    V(lambda: nc.vector.memset(carry[:], 0.0), [], b_carry)

                    xT = sbt(pbk, "xT_b", (128, 8, 256), BF16)
                    b_xT = Buf("xTb")
                    pB = sbt(pbk, "pB", (128, 3, 1 + 256), F32)
                    xm = sbt(pbk, "xm", (128, 3, 256), F32)
                    b_pB = [Buf("pB%d" % i) for i in range(3)]
                    b_xm = [Buf("xm%d" % i) for i in range(3)]
                    lwa = sbt(pbk, "lwa", (128, 256), BF16)
                    sgd = sbt(pbk, "sgd", (128, 256), BF16)
                    b_lwa = Buf("lwa")
                    b_sgd = Buf("sgd")
                    names = ["lw", "Lc", "aa", "kkn", "kap", "E0", "Em", "Ee", "t1", "t2"]
                    ft = {n: sbt(pbk, "f_" + n, (128, 256), F32) for n in names}
                    fb = {n: Buf("f_" + n) for n in names}
                    Bh = sbt(pbk, "Bh", (128, 256), BF16)
                    Kh = sbt(pbk, "Kh", (128, 256), BF16)
                    xvb = sbt(pbk, "xvb", (128, 256), BF16)
                    sqb = sbt(pbk, "sqb", (128, 256), BF16)
                    b_Bh, b_Kh, b_xvb, b_sqb = [Buf(n) for n in ("Bh", "Kh", "xvb", "sqb")]
                    oAR, oARm, oBt, oBtm, oKt, oTM3, oE1, obon, ogt = [[] for _ in range(9)]
                    b_oAR, b_oARm, b_oBt, b_oBtm, b_oKt, b_oTM3, b_oE1, b_obon, b_ogt = [[] for _ in range(9)]

                    def alloc_set(st):
                        i = len(oAR)
                        oAR.append(sbt(st, "oAR", (128, 2, 2, 128), BF16))
                        oARm.append(sbt(st, "oARm", (128, 2, 2, 2, 128), BF16))
                        oBt.append(sbt(st, "oBt", (128, 256), BF16))
                        oBtm.append(sbt(st, "oBtm", (128, 2, 256), BF16))
                        oKt.append(sbt(st, "oKt", (128, 256), BF16))
                        oTM3.append(sbt(st, "oTM3", (128, 2, 384), BF16))
                        oE1.append(sbt(st, "oE1", (128, 256), F32))
                        obon.append(sbt(st, "obon", (128, 256), F32))
                        ogt.append(sbt(st, "ogt", (128, 256), F32))
                        for lst, nm in ((b_oAR, "AR"), (b_oARm, "ARm"), (b_oBt, "Bt"), (b_oBtm, "Btm"), (b_oKt, "Kt"), (b_oTM3, "TM3"),
                                        (b_oE1, "E1"), (b_obon, "bon"), (b_ogt, "gt")):
                            lst.append(Buf("%s%d" % (nm, i)))

                    cLM1, cLM2, cL0, cNL, cQ = [], [], [], [], []
                    b_cLM1, b_cLM2, b_cL0, b_cNL, b_cQ = [], [], [], [], []

                    def alloc_chain(st):
                        i = len(cLM1)
                        cLM1.append(sbt(st, "cLM1", (128, 2, 256), BF16))
                        cLM2.append(sbt(st, "cLM2", (128, 2, 256), BF16))
                        cL0.append(sbt(st, "cL0", (128, 2, 128), BF16))
                        cNL.append([sbt(st, "cNL", (128, 2, 2, 128), BF16) for _ in range(2)])
                        cQ.append([sbt(st, "cQ", (128, 2, 128), BF16) for _ in range(2)])
                        b_cLM1.append(Buf("cLM1_%d" % i))
                        b_cLM2.append(Buf("cLM2_%d" % i))
                        b_cL0.append(Buf("cL0_%d" % i))
                        b_cNL.append([Buf("cNL%d_%d" % (i, k)) for k in range(2)])
                        b_cQ.append([Buf("cQ%d_%d" % (i, k)) for k in range(2)])

                    alloc_set(pbk)
                    alloc_chain(pbk)
                    gnt = [ft["t1"], ft["t2"], ft["E0"]]
                    b_gnt = [fb["t1"], fb["t2"], fb["E0"]]
                    Xs = sbt(pbk, "Xs", (128, 2, 64), BF16)
                    Ub = sbt(pbk, "Ub", (128, 2, 64), BF16)
                    b_Xs, b_Ub = Buf("Xs"), Buf("Ub")
                    bT = sbt(pbk, "bT", (128, 4, 256), BF16)
                    b_bT = Buf("bT")

                    smp = ExitStack()
                    mu_t["s"] = sbt(smp, "mu_s", (128, 256), F32)
                    msk["s"] = sbt(smp, "ml_s", (128, 128), F32)
                    S.dma("sp", mu_t["s"][:], cd["c_mu_s"], writes=[b_cm])
                    S.dma("sp", msk["s"][:], cd["c_ml_s"], writes=[b_cm])
                    colmask = sbt(smp, "colmask", (128, 16, 128), BF16)
                    rowmask = sbt(smp, "rowmask", (128, 16), F32)
                    S.dma("pool", colmask[:], cd["c_colmask"], writes=[b_c])
                    S.dma("sp", rowmask[:], cd["c_rowmask"], writes=[b_c])
                    Hsf = sbt(smp, "Hsf", (128, 16, 64), F32)
                    Hsb = sbt(smp, "Hsb", (128, 16, 64), BF16)
                    scarry = sbt(smp, "scarry", (128, 14, 16), F32)
                    smask = sbt(smp, "smask", (128, 8, 128), BF16)
                    b_smask = Buf("smask")
                    wst = sbt(smp, "wst", (64, 8, 128), F32)
                    b_wst = Buf("wst")
                    stg = sbt(smp, "stg", (16, 512), F32)
                    b_stg = Buf("stg")
                    for q in range(4):
                        w_ = 512 if q < 3 else 256
                        S.dma("sp", stg[0:16, 0:w_], sshift[:, q * 512:q * 512 + w_], writes=[b_stg])
                        for r in range(w_ // 128):
                            ci = q * 4 + r
                            T(lambda: nc.tensor.transpose(PS[:, 0, ci * 16:ci * 16 + 16], stg[0:16, r * 128:(r + 1) * 128], ident[0:16, 0:16]),
                              [b_stg, b_c], [pb[0]])
                    V(lambda: nc.vector.tensor_copy(scarry[:], PS[:, 0, 0:224].rearrange("p (c s) -> p c s", c=14)), [pb[0]], b_carry)

                    def load_sample_state(cc):
                        for half in range(2):
                            bk = 5 + half
                            for h2 in range(2):
                                S.dma("sp", wst[:, :, h2 * 64:(h2 + 1) * 64], swkv[half * 8:(half + 1) * 8, 2 * cc + h2, :, :].rearrange("s v k -> v s k"), writes=[b_wst])
                            for q in range(8):
                                T(lambda: nc.tensor.transpose(PS[:, bk, q * 64:(q + 1) * 64], wst[:, q, :], ident[0:64, 0:64]), [b_wst, b_c], [pb[bk]])
                            V(lambda: nc.vector.tensor_copy(Hsf[:, half * 8:(half + 1) * 8, :], PS[:, bk, :].rearrange("p (q v) -> p q v", q=8)),
                              [pb[bk]], [b_Hs[cc]])
                        A(lambda: nc.scalar.copy(Hsb[:], Hsf[:]), [b_Hs[cc]], [b_Hs[cc]])

                    def store_sample_state(cc):
                        for half in range(2):
                            for q2 in range(2):
                                bk = 5 + q2
                                for r in range(4):
                                    s_ = half * 8 + q2 * 4 + r
                                    T(lambda: nc.tensor.transpose(PS[0:64, bk, r * 128:(r + 1) * 128], Hsf[:, s_, :], ident[:]), [b_Hs[cc], b_c], [pb[bk]])
                                V(lambda: nc.vector.tensor_copy(wst[:, q2 * 4:(q2 + 1) * 4, :], PS[0:64, bk, :].rearrange("p (r x) -> p r x", r=4)), [pb[bk]], [b_wst])
                            for h2 in range(2):
                                odma(o_wkv_s[half * 8:(half + 1) * 8, 2 * cc + h2, :, :].rearrange("s v k -> v s k"), wst[:, :, h2 * 64:(h2 + 1) * 64], [b_wst])

                    def shifted_chunk(slot, ci, bank, ntok, sample, off=0, pbuf=None):
                        pbuf = pb[bank] if pbuf is None else pbuf
                        if not sample:
                            A(lambda: nc.scalar.copy(pB[:, slot, 1:1 + ntok], PS[:, bank, off:off + ntok]), [pbuf], [b_pB[slot]])
                            V(lambda: nc.vector.tensor_copy(pB[:, slot, 0:1], carry[:, ci:ci + 1]), [b_carry[ci]], [b_pB[slot]])
                            V(lambda: nc.vector.tensor_copy(carry[:, ci:ci + 1], pB[:, slot, ntok:ntok + 1]), [b_pB[slot]], [b_carry[ci]])
                            cur = pB[:, slot, 1:1 + ntok]
                            prv = pB[:, slot, 0:ntok]
                            dst = xm[:, slot, 0:ntok]
                            psrc = PS[:, bank, off:off + ntok]
                        else:
                            v3 = pB[:, slot, 0:144].rearrange("p (q r) -> p q r", q=16)
                            A(lambda: nc.scalar.copy(v3[:, :, 1:9], PS[:, bank, off:off + 128].rearrange("p (q r) -> p q r", q=16)), [pbuf], [b_pB[slot]])
                            V(lambda: nc.vector.tensor_copy(v3[:, :, 0], scarry[:, ci, :]), [b_carry[ci]], [b_pB[slot]])
                            V(lambda: nc.vector.tensor_copy(scarry[:, ci, :], v3[:, :, 8]), [b_pB[slot]], [b_carry[ci]])
                            cur = v3[:, :, 1:9]
                            prv = v3[:, :, 0:8]
                            dst = xm[:, slot, 0:128].rearrange("p (q r) -> p q r", q=16)
                            psrc = PS[:, bank, off:off + 128].rearrange("p (q r) -> p q r", q=16)
                        A(lambda: nc.scalar.activation(dst, psrc, AF.Identity, scale=omm[:, ci:ci + 1]), [pbuf, b_pv], [b_xm[slot]])
                        V(lambda: nc.vector.scalar_tensor_tensor(dst, prv, pmu[:, ci:ci + 1], dst, ALU.mult, ALU.add),
                          [b_xm[slot], b_pB[slot], b_pv], [b_xm[slot]])

                    def proj_chunk(ci, bank, ntok, off=0, pbuf=None):
                        pbuf = pb[bank] if pbuf is None else pbuf
                        for kc in range(8):
                            T(lambda: nc.tensor.matmul(PS[:, bank, off:off + ntok], winB[:, kc, ci * 128:(ci + 1) * 128], xT[:, kc, 0:ntok],
                                                       start=(kc == 0), stop=(kc == 7)), [b_wBc[ci], b_xT], [pbuf])

                    GROUPS_B = [([16], 128)] + [([2 * i, 2 * i + 1], 256) for i in range(8)]
                    Hf = Hb = rm_p = None
                    def sample_to_prompt_transition():
                        nonlocal Hf, Hb, rm_p, gnt, b_gnt
                        for ci in range(14):
                            T(lambda: nc.tensor.transpose(PS[0:16, 4 + ci // 4, (ci % 4) * 128:(ci % 4 + 1) * 128], scarry[:, ci, :], ident[:]),
                              [b_carry[ci], b_c], [pb[4 + ci // 4]])
                        for q in range(4):
                            w_ = 512 if q < 3 else 256
                            V(lambda: nc.vector.tensor_copy(stg[0:16, 0:w_], PS[0:16, 4 + q, 0:w_]), [pb[4 + q]], [b_stg])
                            odma(o_shift_s[:, q * 512:q * 512 + w_], stg[0:16, 0:w_], [b_stg])
                        checkpoint(3) if q == 3 else None
                        S.barrier()
                        smp.close()
                        Hf = sbt(pbk, "Hf", (128, 4, 64), F32)
                        Hb = sbt(pbk, "Hb", (128, 4, 64), BF16)
                        V(lambda: nc.vector.memset(Hf[:], 0.0), [], b_H)
                        V(lambda: nc.vector.memset(Hb[:], 0.0), [], b_H)
                        mu_t["p"] = sbt(pbk, "mu_p", (128, 256), F32)
                        msk["p"] = sbt(pbk, "ml_p", (128, 128), F32)
                        rm_p = sbt(pbk, "rm_p", (128, 256), F32)
                        alloc_set(pbk)
                        alloc_chain(pbk)
                        gnt = [lntmp[:, 256 * i:256 * (i + 1)] for i in range(3)]
                        b_gnt = b_lnq[0:3]
                        print("sbuf bytes remaining in prompt scope:", nc.sbuf_bytes_remaining)
                        S.dma("sp", mu_t["p"][:], cd["c_mu_p"], writes=[b_cm])
                        S.dma("sp", msk["p"][:], cd["c_ml_p"], writes=[b_cm])
                        S.dma("sp", rm_p[:], cd["c_rm_p"][:, 0:256], writes=[b_cm])
                    def group_fns(gi, tiles, ntok):
                        sample = (gi == 0)
                        nt = len(tiles)
                        mk = "s" if sample else "p"
                        def gen_pre():
                            for j_, ti_ in enumerate(tiles):
                                make_xT(xT, b_xT, [ti_], col0=j_ * 128)
                                yield
                            proj_chunk(12, 6, ntok)
                            shifted_chunk(0, 12, 6, ntok, sample)
                            A(lambda: nc.scalar.activation(lwa[0:64, 0:ntok], xm[0:64, 0, 0:ntok], AF.Tanh), [b_xm[0]], [b_lwa])
                            A(lambda: nc.scalar.copy(lwa[64:128, 0:ntok], xm[64:128, 0, 0:ntok]), [b_xm[0]], [b_lwa])
                            yield
                            proj_chunk(13, 7, ntok)
                            shifted_chunk(1, 13, 7, ntok, sample)
                            A(lambda: nc.scalar.activation(sgd[:, 0:ntok], xm[:, 1, 0:ntok], AF.Sigmoid), [b_xm[1]], [b_sgd])
                            yield
                            if not sample:
                                yield from gen_prep(0, 0)

                        def gen_prep(cc, bs):
                            f = {n: ft[n][:, 0:ntok] for n in names}
                            E1 = oE1[bs][:, 0:ntok]
                            bon = obon[bs][:, 0:ntok]
                            gt = ogt[bs][:, 0:ntok]
                            if sample:
                                load_sample_state(cc)
                            slots = ((6, 0, ph[6][0]), (6, ntok, ph[6][1]), (7, 0, ph[7][0]))
                            for slot, base in enumerate((0, 4, 8)):
                                bk_, off, pbf = slots[slot]
                                proj_chunk(base + cc, bk_, ntok, off, pbf)
                                shifted_chunk(slot, base + cc, bk_, ntok, sample, off, pbf)
                                yield
                            xr, xk, xv = xm[:, 0, 0:ntok], xm[:, 1, 0:ntok], xm[:, 2, 0:ntok]
                            pw_ = PS[:, 7, ntok:2 * ntok]
                            T(lambda: nc.tensor.matmul(pw_, lw2[0:64, cc * 128:(cc + 1) * 128], lwa[0:64, 0:ntok], start=True, stop=True),
                              [b_wB, b_lwa], [ph[7][1]])
                            A(lambda: nc.scalar.activation(f["lw"], pw_, AF.Sigmoid, bias=pv[:, cc, W0:W0 + 1]), [ph[7][1], b_pv], [fb["lw"]])
                            pa_ = PS[:, 6, 0:ntok]
                            T(lambda: nc.tensor.matmul(pa_, lw2[64:128, cc * 128:(cc + 1) * 128], lwa[64:128, 0:ntok], start=True, stop=True),
                              [b_wB, b_lwa], [ph[6][0]])
                            A(lambda: nc.scalar.activation(f["aa"], pa_, AF.Sigmoid, bias=pv[:, cc, A0:A0 + 1]), [ph[6][0], b_pv], [fb["aa"]])
                            yield
                            pg_ = PS[:, 6, ntok:2 * ntok]
                            T(lambda: nc.tensor.matmul(pg_, lg2[:, cc * 128:(cc + 1) * 128], sgd[:, 0:ntok], start=True, stop=True),
                              [b_wB, b_sgd], [ph[6][1]])
                            A(lambda: nc.scalar.copy(gt, pg_), [ph[6][1]], [b_ogt[bs]])
                            rm = rm_s[:, 0:128] if sample else rm_p[:, 0:ntok]
                            V(lambda: nc.vector.tensor_tensor_scan(f["Lc"], rm, f["lw"], 0.0, ALU.mult, ALU.add), [fb["lw"], b_c, b_cm], [fb["Lc"]])
                            yield
                            A(lambda: nc.scalar.activation(E1, f["Lc"], AF.Exp, scale=DECAY_C), [fb["Lc"]], [b_oE1[bs]])
                            A(lambda: nc.scalar.activation(f["Em"], f["Lc"], AF.Exp, scale=-DECAY_C), [fb["Lc"]], [fb["Em"]])
                            V(lambda: nc.vector.tensor_tensor(f["t1"], f["Lc"], f["lw"], ALU.subtract), [fb["Lc"], fb["lw"]], [fb["t1"]])
                            A(lambda: nc.scalar.activation(f["E0"], f["t1"], AF.Exp, scale=DECAY_C), [fb["t1"]], [fb["E0"]])
                            yield
                            if not sample:
                                for j in range(nt):
                                    sl = slice(j * 128, (j + 1) * 128)
                                    V(lambda: nc.vector.tensor_scalar(lend[:, j:j + 1], ft["Lc"][:, j * 128 + 127:j * 128 + 128], DECAY_C, None, ALU.mult),
                                      [fb["Lc"]], [b_lend])
                                    A(lambda: nc.scalar.activation(ft["Ee"][:, sl], ft["Lc"][:, sl], AF.Exp, scale=-DECAY_C,
                                                                   bias=lend[:, j:j + 1]), [fb["Lc"], b_lend], [fb["Ee"]])
                            else:
                                L3 = ft["Lc"][:, 0:128].rearrange("p (q r) -> p q r", q=16)
                                V(lambda: nc.vector.tensor_tensor(ft["t2"][:, 0:128].rearrange("p (q r) -> p q r", q=16),
                                                                  L3[:, :, 7:8].to_broadcast([128, 16, 8]), L3, ALU.subtract), [fb["Lc"]], [fb["t2"]])
                                A(lambda: nc.scalar.activation(f["Ee"], f["t2"], AF.Exp, scale=DECAY_C), [fb["t2"]], [fb["Ee"]])
                            A(lambda: nc.scalar.activation(f["kkn"], xk, AF.Identity, scale=pv[:, cc, KK:KK + 1]), [b_xm[1], b_pv], [fb["kkn"]])
                            V(lambda: nc.vector.tensor_tensor(sqb[:, 0:ntok], f["kkn"], f["kkn"], ALU.mult), [fb["kkn"]], [b_sqb])
                            yield
                            pss = PS[:, 7, 0:ntok]
                            T(lambda: nc.tensor.matmul(pss, bonesb[:], sqb[:, 0:ntok], start=True, stop=True), [b_c, b_sqb], [ph[7][0]])
                            V(lambda: nc.vector.tensor_scalar(f["t1"], pss, 1e-24, None, ALU.max), [ph[7][0]], [fb["t1"]])
                            A(lambda: nc.scalar.activation(f["t1"], f["t1"], AF.Ln), [fb["t1"]], [fb["t1"]])
                            A(lambda: nc.scalar.activation(f["t1"], f["t1"], AF.Exp, scale=-0.5), [fb["t1"]], [fb["t1"]])
                            yield
                            V(lambda: nc.vector.tensor_tensor(f["kkn"], f["kkn"], f["t1"], ALU.mult), [fb["kkn"], fb["t1"]], [fb["kkn"]])
                            A(lambda: nc.scalar.activation(f["t2"], f["aa"], AF.Identity, scale=pv[:, cc, KA:KA + 1], bias=omka[:, cc:cc + 1]),
                              [fb["aa"], b_pv], [fb["t2"]])
                            V(lambda: nc.vector.tensor_tensor(f["kap"], xk, f["t2"], ALU.mult), [b_xm[1], fb["t2"]], [fb["kap"]])
                            yield
                            AR4 = oAR[bs][:, 0:nt, :, :]
                            v3 = lambda ap: ap.rearrange("p (i t) -> p i t", i=nt)
                            V(lambda: nc.vector.scalar_tensor_tensor(AR4[:, :, 0, :], v3(f["kkn"]), -1.0, v3(f["E0"]), ALU.mult, ALU.mult),
                              [fb["kkn"], fb["E0"]], [b_oAR[bs]])
                            V(lambda: nc.vector.tensor_tensor(AR4[:, :, 1, :], v3(xr), v3(E1), ALU.mult), [b_xm[0], b_oE1[bs]], [b_oAR[bs]])
                            V(lambda: nc.vector.tensor_tensor(f["t1"], f["kkn"], f["aa"], ALU.mult), [fb["kkn"], fb["aa"]], [fb["t1"]])
                            yield
                            V(lambda: nc.vector.tensor_tensor(oBt[bs][:, 0:ntok], f["t1"], f["Em"], ALU.mult), [fb["t1"], fb["Em"]], [b_oBt[bs]])
                            V(lambda: nc.vector.tensor_tensor(Bh[:, 0:ntok], f["t1"], f["Ee"], ALU.mult), [fb["t1"], fb["Ee"]], [b_Bh])
                            V(lambda: nc.vector.tensor_tensor(oKt[bs][:, 0:ntok], f["kap"], f["Em"], ALU.mult), [fb["kap"], fb["Em"]], [b_oKt[bs]])
                            V(lambda: nc.vector.tensor_tensor(Kh[:, 0:ntok], f["kap"], f["Ee"], ALU.mult), [fb["kap"], fb["Ee"]], [b_Kh])
                            yield
                            for h2 in range(2):
                                A(lambda: nc.scalar.activation(oARm[bs][:, h2, 0:nt, :, :], AR4, AF.Identity, scale=headmask[:, h2:h2 + 1]), [b_oAR[bs], b_c], [b_oARm[bs]])
                                A(lambda: nc.scalar.activation(oBtm[bs][:, h2, 0:ntok], oBt[bs][:, 0:ntok], AF.Identity, scale=headmask[:, h2:h2 + 1]), [b_oBt[bs], b_c], [b_oBtm[bs]])
                            A(lambda: nc.scalar.copy(xvb[:, 0:ntok], xv), [b_xm[2]], [b_xvb])
                            yield
                            V(lambda: nc.vector.scalar_tensor_tensor(sqb[:, 0:ntok], xr, pv[:, cc, RK:RK + 1], f["kap"], ALU.mult, ALU.mult),
                              [b_xm[0], fb["kap"], b_pv], [b_sqb])
                            prk = PS[:, 7, ntok:2 * ntok]
                            T(lambda: nc.tensor.matmul(prk, bonesb[:], sqb[:, 0:ntok], start=True, stop=True), [b_c, b_sqb], [ph[7][1]])
                            V(lambda: nc.vector.tensor_tensor(bon, prk, xv, ALU.mult), [ph[7][1], b_xm[2]], [b_obon[bs]])
                            yield
                            pst = PS[:, 6, :].bitcast(BF16)
                            tmb = lambda j: pb[6] if sample else ph[6][j]
                            for j in range(nt):
                                sl = slice(j * 128, (j + 1) * 128)
                                for q, (src, bsrc) in enumerate(((xvb, b_xvb), (Bh, b_Bh), (Kh, b_Kh))):
                                    T(lambda: nc.tensor.transpose(pst[:, j * 512 + q * 128:j * 512 + (q + 1) * 128], src[:, sl], identb[:]), [bsrc, b_c], [tmb(j)])
                                V(lambda: nc.vector.tensor_copy(oTM3[bs][:, j, :], pst[:, j * 512:j * 512 + 384]), [tmb(j)], [b_oTM3[bs]])
                                yield

                        def gen_mats(cc, bs, j):
                            sl = slice(j * 128, (j + 1) * 128)
                            rhsAR = oARm[bs][:, :, j, :, :].rearrange("p h a t -> p h (a t)")
                            T(lambda: nc.tensor.matmul(PS[:, 2, :], oKt[bs][:, sl], rhsAR, start=True, stop=True), [b_oKt[bs], b_oARm[bs]], [pb[2]])
                            T(lambda: nc.tensor.matmul(PS[:, 3, :], oBt[bs][:, sl], rhsAR, start=True, stop=True), [b_oBt[bs], b_oARm[bs]], [pb[3]])
                            T(lambda: nc.tensor.matmul(PS[:, 4, 0:256], oAR[bs][:, j, 0, :], oBtm[bs][:, :, sl], start=True, stop=True), [b_oAR[bs], b_oBtm[bs]], [ph[4][0]])
                            mub = mu_t[mk][:].unsqueeze(1).to_broadcast([128, 2, 256])
                            V(lambda: nc.vector.tensor_tensor(cLM1[j][:], PS[:, 2, :].rearrange("p (h x) -> p h x", h=2), mub, ALU.mult), [pb[2], b_cm], [b_cLM1[j]])
                            V(lambda: nc.vector.tensor_tensor(cLM2[j][:], PS[:, 3, :].rearrange("p (h x) -> p h x", h=2), mub, ALU.mult), [pb[3], b_cm], [b_cLM2[j]])
                            V(lambda: nc.vector.tensor_tensor(cL0[j][:], PS[:, 4, 0:256].rearrange("p (h x) -> p h x", h=2),
                                                              msk[mk][:].unsqueeze(1).to_broadcast([128, 2, 128]), ALU.mult), [ph[4][0], b_cm], [b_cL0[j]])
                            yield

                        def gen_chain(cc, bs, j, res):
                            if j == 0:
                                nlb, nlbuf, qap, qbuf = 5, pb[5], PS[:, 4, 256:512], ph[4][1]
                            else:
                                nlb, nlbuf, qap, qbuf = 2, pb[2], PS[:, 3, 0:256], ph[3][0]
                            Np = lambda h2: cLM2[j][:, h2, 0:128]
                            Lp = lambda h2: cL0[j][:, h2, :]
                            bNp, bLp = b_cLM2[j], b_cL0[j]
                            Qp = None
                            bQp = b_c
                            nlev = 3 if sample else NLEV
                            for lev in range(1, nlev + 2):
                                last = (lev == nlev + 1)
                                pp = lev % 2
                                for h2 in range(2):
                                    if not last and lev < nlev:
                                        T(lambda: nc.tensor.matmul(PS[:, nlb, h2 * 128:(h2 + 1) * 128], Lp(h2), Np(h2), start=True, stop=True), [bLp, bNp], [nlbuf])
                                    if not last:
                                        T(lambda: nc.tensor.matmul(PS[:, nlb, 256 + h2 * 128:256 + (h2 + 1) * 128], Np(h2), Lp(h2), start=True, stop=True), [bLp, bNp], [nlbuf])
                                    qrhs = identb[:] if Qp is None else Qp(h2)
                                    T(lambda: nc.tensor.matmul(qap[:, h2 * 128:(h2 + 1) * 128], Lp(h2), qrhs, start=True, stop=True), [bLp, bQp], [qbuf])
                                if Qp is None:
                                    V(lambda: nc.vector.tensor_tensor(cQ[j][pp][:], qap.rearrange("p (h x) -> p h x", h=2),
                                                                      identb[:].unsqueeze(1).to_broadcast([128, 2, 128]), ALU.add), [qbuf, b_c], [b_cQ[j][pp]])
                                else:
                                    V(lambda: nc.vector.tensor_tensor(cQ[j][pp][:], qap.rearrange("p (h x) -> p h x", h=2), cQ[j][1 - pp][:], ALU.add),
                                      [qbuf, b_cQ[j][1 - pp]], [b_cQ[j][pp]])
                                if not last:
                                    A(lambda: nc.scalar.copy(cNL[j][pp][:], PS[:, nlb, :].rearrange("p (a h x) -> p a h x", a=2, h=2)), [nlbuf], [b_cNL[j][pp]])
                                    Np = (lambda pp_: (lambda h2: cNL[j][pp_][:, 0, h2, :]))(pp)
                                    Lp = (lambda pp_: (lambda h2: cNL[j][pp_][:, 1, h2, :]))(pp)
                                    bNp = bLp = b_cNL[j][pp]
                                Qp = (lambda pp_: (lambda h2: cQ[j][pp_][:, h2, :]))(pp)
                                bQp = b_cQ[j][pp]
                                yield
                            res[j] = (Qp, bQp)

                        def gen_state(cc, bs, j, res):
                            sl = slice(j * 128, (j + 1) * 128)
                            TT, bTT = res[j]
                            LM1, LM2, b_LM1, b_LM2 = cLM1[j], cLM2[j], b_cLM1[j], b_cLM2[j]
                            ARm, TM3, b_ARm, b_TM3 = oARm[bs], oTM3[bs], b_oARm[bs], b_oTM3[bs]
                            for h2 in range(2):
                                hs = slice(h2 * 64, (h2 + 1) * 64)
                                xo = PS[:, 0, h2 * 64:(h2 + 1) * 64]
                                if not sample:
                                    T(lambda: nc.tensor.matmul(xo, ARm[:, h2, j, 0, :], Hb[:, cc, :], start=True, stop=False), [b_ARm, b_H[cc]], [ph[0][0]])
                                else:
                                    for half in range(2):
                                        V(lambda: nc.vector.tensor_tensor(smask[:], ARm[:, h2, 0, 0, :].unsqueeze(1).to_broadcast([128, 8, 128]),
                                                                          colmask[:, half * 8:(half + 1) * 8, :], ALU.mult), [b_ARm, b_c], [b_smask])
                                        for q in range(8):
                                            s = half * 8 + q
                                            T(lambda: nc.tensor.matmul(xo, smask[:, q, :], Hsb[:, s, :], start=(s == 0), stop=False), [b_smask, b_Hs[cc]], [ph[0][0]])
                                T(lambda: nc.tensor.matmul(xo, LM1[:, h2, 0:128], TM3[:, j, hs], start=False, stop=True), [b_LM1, b_TM3], [ph[0][0]])
                            V(lambda: nc.vector.tensor_copy(Xs[:], PS[:, 0, 0:128].rearrange("p (h v) -> p h v", h=2)), [ph[0][0]], [b_Xs])
                            yield
                            for h2 in range(2):
                                T(lambda: nc.tensor.matmul(PS[:, 0, 128 + h2 * 64:128 + (h2 + 1) * 64], TT(h2), Xs[:, h2, :], start=True, stop=True), [bTT, b_Xs], [ph[0][0]])
                            V(lambda: nc.vector.tensor_copy(Ub[:], PS[:, 0, 128:256].rearrange("p (h v) -> p h v", h=2)), [ph[0][0]], [b_Ub])
                            yield
                            for h2 in range(2):
                                hs = slice(h2 * 64, (h2 + 1) * 64)
                                oo = PS[hs, 1, sl]
                                if not sample:
                                    T(lambda: nc.tensor.matmul(oo, Hb[:, cc, :], ARm[:, h2, j, 1, :], start=True, stop=False), [b_H[cc], b_ARm], [pb[1]])
                                else:
                                    for s in range(16):
                                        T(lambda: nc.tensor.matmul(PS[hs, 1, s * 8:(s + 1) * 8], Hsb[:, s, :], ARm[:, h2, 0, 1, s * 8:(s + 1) * 8],
                                                                   start=(s == 0), stop=False, skip_group_check=True), [b_Hs[cc], b_ARm], [pb[1]])
                                T(lambda: nc.tensor.matmul(oo, Ub[:, h2, :], LM2[:, h2, 128:256], start=False, stop=False, skip_group_check=sample), [b_Ub, b_LM2], [pb[1]])
                                T(lambda: nc.tensor.matmul(oo, TM3[:, j, hs], LM1[:, h2, 128:256], start=False, stop=True, skip_group_check=sample), [b_TM3, b_LM1], [pb[1]])
                                if not sample:
                                    ho = PS[hs, 0, 256:320]
                                    T(lambda: nc.tensor.matmul(ho, TM3[:, j, 128 + h2 * 64:128 + (h2 + 1) * 64], Ub[:, h2, :], start=True, stop=False), [b_TM3, b_Ub], [ph[0][1]])
                                    T(lambda: nc.tensor.matmul(ho, TM3[:, j, 256 + h2 * 64:256 + (h2 + 1) * 64], TM3[:, j, hs], start=False, stop=True), [b_TM3], [ph[0][1]])
                                else:
                                    sm4 = smask[:].rearrange("p s (a v) -> p a s v", a=2)
                                    for half in range(2):
                                        rmb = rowmask[:, half * 8:(half + 1) * 8].unsqueeze(2).to_broadcast([128, 8, 64])
                                        V(lambda: nc.vector.tensor_tensor(sm4[:, 0, :, :], Ub[:, h2, :].unsqueeze(1).to_broadcast([128, 8, 64]), rmb, ALU.mult),
                                          [b_Ub, b_c], [b_smask])
                                        V(lambda: nc.vector.tensor_tensor(sm4[:, 1, :, :], TM3[:, 0, hs].unsqueeze(1).to_broadcast([128, 8, 64]), rmb, ALU.mult),
                                          [b_TM3, b_c], [b_smask])
                                        for q in range(8):
                                            ho = PS[hs, 2 + half, q * 64:(q + 1) * 64]
                                            T(lambda: nc.tensor.matmul(ho, TM3[:, 0, 128 + h2 * 64:128 + (h2 + 1) * 64], sm4[:, 0, q, :], start=True, stop=False),
                                              [b_TM3, b_smask], [pb[2 + half]])
                                            T(lambda: nc.tensor.matmul(ho, TM3[:, 0, 256 + h2 * 64:256 + (h2 + 1) * 64], sm4[:, 1, q, :], start=False, stop=True),
                                              [b_TM3, b_smask], [pb[2 + half]])
                            if not sample:
                                V(lambda: nc.vector.scalar_tensor_tensor(Hf[:, cc, :], Hf[:, cc, :], oE1[bs][:, j * 128 + 127:j * 128 + 128], PS[:, 0, 256:320],
                                                                         ALU.mult, ALU.add), [b_H[cc], b_oE1[bs], ph[0][1]], [b_H[cc]])
                                A(lambda: nc.scalar.copy(Hb[:, cc, :], Hf[:, cc, :]), [b_H[cc]], [b_H[cc]])
                            else:
                                gcs = oE1[bs][:, 0:128].rearrange("p (q r) -> p q r", q=16)[:, :, 7:8].to_broadcast([128, 16, 64])
                                V(lambda: nc.vector.tensor_tensor(Hsf[:], Hsf[:], gcs, ALU.mult), [b_Hs[cc], b_oE1[bs]], [b_Hs[cc]])
                                for a_ in range(2):
                                    V(lambda: nc.vector.tensor_tensor(Hsf[:, a_ * 8:(a_ + 1) * 8, :], Hsf[:, a_ * 8:(a_ + 1) * 8, :],
                                                                      PS[:, 2 + a_, :].rearrange("p (q v) -> p q v", q=8), ALU.add),
                                      [b_Hs[cc], pb[2 + a_]], [b_Hs[cc]])
                                store_sample_state(cc)
                            yield

                        def gen_gn(cc, bs):
                            g1, g2, oT_ = (gnt[0][:, 0:ntok], gnt[1][:, 0:ntok], gnt[2][:, 0:ntok])
                            bg1, bg2, boT = b_gnt
                            A(lambda: nc.scalar.copy(oT_, PS[:, 1, 0:ntok]), [pb[1]], [boT])
                            V(lambda: nc.vector.tensor_tensor(g1, oT_, oT_, ALU.mult), [boT], [bg1])
                            pm_, pq_ = PS[:, 4, 0:ntok], PS[:, 4, 256:256 + ntok]
                            T(lambda: nc.tensor.matmul(pm_, bonesf[:], oT_, start=True, stop=True), [b_c, boT], [ph[4][0]])
                            T(lambda: nc.tensor.matmul(pq_, bonesf[:], g1, start=True, stop=True), [b_c, bg1], [ph[4][1]])
                            yield
                            A(lambda: nc.scalar.mul(g2, pm_, 1.0 / 64), [ph[4][0]], [bg2])
                            V(lambda: nc.vector.tensor_tensor(g1, g2, g2, ALU.mult), [bg2], [bg1])
                            V(lambda: nc.vector.scalar_tensor_tensor(g1, pq_, 1.0 / 64, g1, ALU.mult, ALU.subtract), [ph[4][1], bg1], [bg1])
                            A(lambda: nc.scalar.activation(g1, g1, AF.Ln, bias=epsc[:, 1:2]), [bg1, b_c], [bg1])
                            A(lambda: nc.scalar.activation(g1, g1, AF.Exp, scale=-0.5), [bg1], [bg1])
                            yield
                            V(lambda: nc.vector.tensor_tensor(oT_, oT_, g2, ALU.subtract), [boT, bg2], [boT])
                            V(lambda: nc.vector.tensor_tensor(oT_, oT_, g1, ALU.mult), [boT, bg1], [boT])
                            A(lambda: nc.scalar.activation(oT_, oT_, AF.Identity, scale=pv[:, cc, LXG:LXG + 1], bias=pv[:, cc, LXB:LXB + 1]),
                              [boT, b_pv], [boT])
                            yield
                            V(lambda: nc.vector.tensor_tensor(oT_, oT_, obon[bs][:, 0:ntok], ALU.add), [boT, b_obon[bs]], [boT])
                            V(lambda: nc.vector.tensor_tensor(bT[:, cc, 0:ntok], oT_, ogt[bs][:, 0:ntok], ALU.mult), [boT, b_ogt[bs]], [b_bT])
                            yield

                        def gen_units(cc, bs):
                            res = {}
                            for j in range(nt):
                                yield from gen_mats(cc, bs, j)
                            if nt == 2:
                                yield from merge(gen_chain(cc, bs, 0, res), gen_chain(cc, bs, 1, res))
                            else:
                                yield from gen_chain(cc, bs, 0, res)
                            for j in range(nt):
                                yield from gen_state(cc, bs, j, res)
                            yield from gen_gn(cc, bs)

                        def gen_body():
                            if sample:
                                for cc in range(4):
                                    yield from gen_prep(cc, 0)
                                    yield from gen_units(cc, 0)
                            else:
                                for cc in range(4):
                                    u = gen_units(cc, cc % 2)
                                    if cc < 3:
                                        yield from merge(u, gen_prep(cc + 1, (cc + 1) % 2))
                                    else:
                                        yield from u

                        def gen_post():
                            for j, ti in enumerate(tiles):
                                mb = 2 + 2 * (j % 2)
                                for hf in range(2):
                                    for kc in range(8):
                                        lhsT = aT_all[:, kc, ti * 128:(ti + 1) * 128] if kc < 4 else bT[:, kc - 4, j * 128:(j + 1) * 128]
                                        T(lambda: nc.tensor.matmul(PS[:, mb + hf, :], lhsT, woutb[:, kc, hf * 512:(hf + 1) * 512], start=(kc == 0), stop=(kc == 7)),
                                          [b_aT, b_bT, b_wB], [pb[mb + hf]])
                                    yield
                                yield from layer_norm_tile(ti, PS[:, mb:mb + 2, :].rearrange("p a b -> p (a b)"), [pb[mb], pb[mb + 1]], gen=True)

                        return gen_pre, gen_body, gen_post

                    fns = [group_fns(gi, tiles, ntok) for gi, (tiles, ntok) in enumerate(GROUPS_B)]
                    run(fns[0][0]())
                    run(fns[0][1]())
                    run(fns[0][2]())
                    sample_to_prompt_transition()
                    run(fns[1][0]())
                    for gi in range(1, len(GROUPS_B)):
                        run(fns[gi][1]())
                        if gi + 1 < len(GROUPS_B):
                            run(merge(fns[gi][2](), fns[gi + 1][0]()))
                        else:
                            run(fns[gi][2]())
                    wkvo = lntmp[0:64, 0:512].rearrange("p (c x) -> p c x", c=4)
                    b_wkvo = Multi(b_lnq[0:2])
                    b_sho1 = Multi(b_lnq[2:4])
                    for q in range(4):
                        n_ = 4 if q < 3 else 2
                        for r in range(n_):
                            ci = q * 4 + r
                            T(lambda: nc.tensor.transpose(PS[0:1, 4 + q, r * 128:(r + 1) * 128], carry[:, ci:ci + 1], ident[:]), [b_carry[ci], b_c], [pb[4 + q]])
                        V(lambda: nc.vector.tensor_copy(lntmp[0:1, 512:512 + n_ * 128], PS[0:1, 4 + q, 0:n_ * 128]), [pb[4 + q]], [b_sho1])
                        odma(o_shift_p[:, q * 512:q * 512 + n_ * 128], lntmp[0:1, 512:512 + n_ * 128], [b_sho1])
                    for cc in range(4):
                        T(lambda: nc.tensor.transpose(PS[0:64, 0, cc * 128:(cc + 1) * 128], Hf[:, cc, :], ident[:]), [b_H[cc], b_c], [pb[0]])
                    V(lambda: nc.vector.tensor_copy(wkvo[:], PS[0:64, 0, :].rearrange("p (c x) -> p c x", c=4)), [pb[0]], [b_wkvo])
                    for cc in range(4):
                        for h2 in range(2):
                            odma(o_wkv_p[2 * cc + h2, :, :], wkvo[:, cc, h2 * 64:(h2 + 1) * 64], [b_wkvo])
                    S.barrier()
                S.barrier()

            def ffn(layer, final):
                with ExitStack() as fs:
                    load_ln(ln_ffn_g, ln_ffn_b, layer)
                    xT = sbt(fs, "xT_f", (128, 8, 640), BF16)
                    hT = sbt(fs, "hT", (128, 22, 640), BF16)
                    wd = sbt(fs, "wd", (128, 22, 1024), BF16)
                    NSL = 3
                    wg = [sbt(fs, "wg%d" % i, (128, 8, 256), BF16) for i in range(NSL)]
                    wu = [sbt(fs, "wu%d" % i, (128, 8, 256), BF16) for i in range(NSL)]
                    sg = [sbt(fs, "sg%d" % i, (128, 512), F32) for i in range(2)]
                    b_xT, b_hT, b_wd = Buf("xTf"), [Buf("hT%d" % i) for i in range(22)], Buf("wd")
                    b_wg = [Buf("wg%d" % i) for i in range(NSL)]
                    slab_state = {"next": 0}

                    def slab_prefetch(upto):
                        while slab_state["next"] <= upto and slab_state["next"] < 44:
                            n_ = slab_state["next"]
                            sl_ = n_ % 11
                            S.dma("pool", wg[n_ % NSL][:], ffn_gate[layer, :, sl_ * 256:(sl_ + 1) * 256].rearrange("(k p) n -> p k n", p=128), writes=[b_wg[n_ % NSL]])
                            S.dma("pool", wu[n_ % NSL][:], ffn_up[layer, :, sl_ * 256:(sl_ + 1) * 256].rearrange("(k p) n -> p k n", p=128), writes=[b_wg[n_ % NSL]], parallel=True)
                            slab_state["next"] = n_ + 1
                    b_sg = [Buf("sg0"), Buf("sg1")]
                    PASSES = ([0, 1, 2, 3], [4, 5, 6, 7], [8, 9, 10, 11], [12, 13, 14, 15, 16])
                    for pi, tiles in enumerate(PASSES):
                        ntk = len(tiles) * 128
                        if pi == 0:
                            slab_prefetch(NSL - 1)
                            make_xT(xT, b_xT, tiles)
                        grp = [(o, min(512, ntk - o)) for o in range(0, ntk, 512)]
                        k = 0
                        for sl in range(11):
                            use_ = pi * 11 + sl
                            wb_ = use_ % NSL
                            slab_prefetch(use_ + NSL - 1)
                            if pi == 0:
                                S.dma("pool", wd[:, 2 * sl:2 * sl + 2, :], ffn_down[layer, sl * 256:(sl + 1) * 256, :].rearrange("(f p) d -> p f d", p=128), writes=[b_wd], parallel=True)
                            for f2 in range(2):
                                fc = sl * 2 + f2
                                for (o, n) in grp:
                                    bg, bu = 2 * (k % 2), 2 * (k % 2) + 1
                                    for kc in range(8):
                                        T(lambda: nc.tensor.matmul(PS[:, bg, 0:n], wg[wb_][:, kc, f2 * 128:(f2 + 1) * 128], xT[:, kc, o:o + n], start=(kc == 0), stop=(kc == 7)),
                                          [b_wg[wb_], b_xT], [pb[bg]])
                                    for kc in range(8):
                                        T(lambda: nc.tensor.matmul(PS[:, bu, 0:n], wu[wb_][:, kc, f2 * 128:(f2 + 1) * 128], xT[:, kc, o:o + n], start=(kc == 0), stop=(kc == 7)),
                                          [b_wg[wb_], b_xT], [pb[bu]])
                                    A(lambda: nc.scalar.activation(sg[k % 2][:, 0:n], PS[:, bg, 0:n], AF.Silu), [pb[bg]], [b_sg[k % 2]])
                                    V(lambda: nc.vector.tensor_tensor(hT[:, fc, o:o + n], sg[k % 2][:, 0:n], PS[:, bu, 0:n], ALU.mult), [b_sg[k % 2], pb[bu]], [b_hT[fc]])
                                    k += 1
                        if pi + 1 < len(PASSES):
                            make_xT(xT, b_xT, PASSES[pi + 1])
                        for j, ti in enumerate(tiles):
                            b0 = 4 + 2 * (j % 2)
                            for hf in range(2):
                                for fc in range(22):
                                    T(lambda: nc.tensor.matmul(PS[:, b0 + hf, :], hT[:, fc, j * 128:(j + 1) * 128], wd[:, fc, hf * 512:(hf + 1) * 512],
                                                               start=(fc == 0), stop=(fc == 21)), [b_hT[fc], b_wd], [pb[b0 + hf]])
                            dst = None
                            if final:
                                dst = y_s if ti == 16 else y_p[ti * 128:(ti + 1) * 128, :]
                            layer_norm_tile(ti, PS[:, b0:b0 + 2, :].rearrange("p a b -> p (a b)"), [pb[b0], pb[b0 + 1]], final_dst=dst)
                    S.barrier()

            checkpoint(4)
            if not os.environ.get("MK_SKIP_FFN0"):
                ffn(0, False)
            checkpoint(5)

            with ExitStack() as l1:
                load_ln(ln_mix_g, ln_mix_b, 1)
                band = sbt(l1, "band", (128, 4, 6, 128), F32)
                pw = sbt(l1, "pw", (128, 4, 2, 256), BF16)
                psc = sbt(l1, "psc", (128, D), F32)
                spl = sbt(l1, "spl", (128, 2, D), F32)
                pT = sbt(l1, "pT", (128, 8, 128), BF16)
                ysc = sbt(l1, "ysc", (128, D), F32)
                b_pc, b_spl, b_pT, b_ysc = Buf("pc"), Buf("spl"), Buf("pT"), Buf("ysc")
                S.dma("sp", band[:], cd["c_band"], writes=[b_pc])
                for g in range(4):
                    S.dma("pool", pw[:, g, :, :], pool_w[g].rearrange("(k p) d -> p k d", p=128), writes=[b_pc])
                bcast_row(psc, pool_scale[0:1, :], [b_pc])
                S.dma("sp", spl[0:120, :, :], spool.rearrange("(h q) r d -> (q r) h d", h=2), writes=[b_spl])
                odma(o_pool_p, x_all[113:128, 15, :], [bx[15]])
                odma(o_pool_s[:, 7:15, :], x_all[:, 16, :], [bx[16]])
                odma(o_pool_s[:, 0:7, :], spool[:, 8:15, :], [])
                for ti in [16] + list(range(15, -1, -1)):
                    for ci in range(8):
                        g = ci // 2
                        cs = slice(ci * 128, (ci + 1) * 128)
                        po = PS[:, ci // 4, (ci % 4) * 128:(ci % 4 + 1) * 128]
                        bk = pb[ci // 4]
                        if ti == 16:
                            T(lambda: nc.tensor.matmul(po, spl[0:120, 0, cs], band[0:120, g, 4, :], start=True, stop=False), [b_spl, b_pc], [bk])
                            T(lambda: nc.tensor.matmul(po, spl[0:120, 1, cs], band[0:120, g, 5, :], start=False, stop=False), [b_spl, b_pc], [bk])
                            T(lambda: nc.tensor.matmul(po, x_all[:, 16, cs], band[:, g, 3, :], start=False, stop=True), [bx[16], b_pc], [bk])
                        elif ti == 0:
                            T(lambda: nc.tensor.matmul(po, x_all[:, 0, cs], band[:, g, 2, :], start=True, stop=True), [bx[0], b_pc], [bk])
                        else:
                            T(lambda: nc.tensor.matmul(po, x_all[:, ti - 1, cs], band[:, g, 1, :], start=True, stop=False), [bx[ti - 1], b_pc], [bk])
                            T(lambda: nc.tensor.matmul(po, x_all[:, ti, cs], band[:, g, 0, :], start=False, stop=True), [bx[ti], b_pc], [bk])
                    A(lambda: nc.scalar.copy(pT[:, 0:4, :], PS[:, 0, :].rearrange("p (c t) -> p c t", c=4)), [pb[0]], [b_pT])
                    V(lambda: nc.vector.tensor_copy(pT[:, 4:8, :], PS[:, 1, :].rearrange("p (c t) -> p c t", c=4)), [pb[1]], [b_pT])
                    for g in range(4):
                        for k2 in range(2):
                            T(lambda: nc.tensor.matmul(PS[:, 2 + g // 2, (g % 2) * 256:(g % 2 + 1) * 256], pT[:, 2 * g + k2, :], pw[:, g, k2, :],
                                                       start=(k2 == 0), stop=(k2 == 1)), [b_pT, b_pc], [pb[2 + g // 2]])
                    V(lambda: nc.vector.tensor_tensor(ysc[:], PS[:, 2:4, :].rearrange("p a b -> p (a b)"), psc[:], ALU.mult), [pb[2], pb[3], b_pc], [b_ysc])
                    layer_norm_tile(ti, ysc[:], [b_ysc])
                S.barrier()

            checkpoint(6)
            ffn(1, True)


        except StopBuild:
            stopped.append(1)

        for b in outbufs:
            if b.w is not None:
                S._wait("sp", b.w[0], b.w[1])
        S.barrier()
        print("instructions", S.nins, "waits", S.nwait)
        if stopped:
            ctx.pop_all()
    return nc, consts


_CACHE = {}


def kernel(**inp):
    f = lambda a: np.ascontiguousarray(np.asarray(a, dtype=np.float32))
    if "nc" not in _CACHE:
        _CACHE["nc"] = build()
    nc, consts = _CACHE["nc"]
    vec_names = ["conv_b", "conv_ln_g", "conv_ln_b", "rwkv_w0", "rwkv_a0", "rwkv_kk", "rwkv_ka", "rwkv_rk", "rwkv_lnx_g", "rwkv_lnx_b"]
    vecs = np.stack([f(inp[n])[0].reshape(512) for n in vec_names], axis=0)
    shared = {
        "w_in": f(inp["w_in"])[0], "conv_w": f(inp["conv_w"])[0], "vecs": f(vecs),
        "mu": f(inp["rwkv_mu"])[0].reshape(14, 128),
        "w2a2": f(np.concatenate([f(inp["rwkv_w2"])[0], f(inp["rwkv_a2"])[0]], axis=0)),
        "g2": f(inp["rwkv_g2"])[0], "w_out": f(inp["w_out"])[0], "pool_w": f(inp["pool_w"])[0],
        "pool_scale": f(inp["pool_scale"]), "ln_mix_g": f(inp["ln_mix_g"]), "ln_mix_b": f(inp["ln_mix_b"]),
        "ffn_gate": f(inp["ffn_gate"]), "ffn_up": f(inp["ffn_up"]), "ffn_down": f(inp["ffn_down"]),
        "ln_ffn_g": f(inp["ln_ffn_g"]), "ln_ffn_b": f(inp["ln_ffn_b"]),
    }
    shared.update(consts)
    xp, xs = f(inp["x_prompt"]), f(inp["x_sample"])
    sc, ss, sw, sp_ = f(inp["state_conv"])[0], f(inp["state_shift"])[0], f(inp["state_wkv"])[0], f(inp["state_pool"])[0]
    in_maps = []
    for c in range(NCORES):
        m = dict(shared)
        q = slice(16 * c, 16 * c + 16)
        m["xp"] = xp[c]
        m["xs"] = xs[q].reshape(128, D)
        m["sconv"] = sc[q]
        m["sshift"] = ss[q]
        m["swkv"] = sw[q]
        m["spool"] = sp_[q]
        in_maps.append(m)
    res = run_bass_kernel_spmd(nc, in_maps, core_ids=list(range(NCORES)))
    R = res.results
    cat = lambda k: np.concatenate([np.asarray(R[c][k], dtype=np.float32) for c in range(NCORES)], axis=0)
    y_prompt = np.stack([np.asarray(R[c]["y_p"], np.float32) for c in range(NCORES)], axis=0)
    y_sample = cat("y_s").reshape(128, 8, D)
    conv_p = np.stack([R[c]["o_conv_p"] for c in range(NCORES)], 0)[None].astype(np.float32)
    shift_p = cat("o_shift_p")[None]
    wkv_p = np.stack([R[c]["o_wkv_p"] for c in range(NCORES)], 0)[None].astype(np.float32)
    pool_p = np.stack([R[c]["o_pool_p"] for c in range(NCORES)], 0)[None].astype(np.float32)
    conv_s = cat("o_conv_s")[None]
    shift_s = cat("o_shift_s")[None]
    wkv_s = cat("o_wkv_s")[None]
    pool_s = cat("o_pool_s")[None]
    return (y_prompt, y_sample, conv_p, shift_p, wkv_p, pool_p, conv_s, shift_s, wkv_s, pool_s)
```

```python
from contextlib import ExitStack
import numpy as np
import concourse.bass as bass
import concourse.mybir as mybir
from concourse.bass_utils import run_bass_kernel_spmd

F32 = mybir.dt.float32
BF16 = mybir.dt.bfloat16
AF = mybir.ActivationFunctionType
ALU = mybir.AluOpType

NCORES = 8
D = 1024
NT = 17
INP = 2816
DFF = 2816
ALPHA = 4 ** 0.25
LN_EPS = 1e-5
GN_EPS = 64e-5
NLEV = 6
DECAY_C = -float(np.exp(-0.5))


import os
STOP = int(os.environ.get("MK_STOP", "99"))


class StopBuild(Exception):
    pass


def checkpoint(n):
    if STOP <= n:
        raise StopBuild()


class Buf:
    __slots__ = ("name", "w", "r", "wx")

    def __init__(self, name):
        self.name = name
        self.w = None
        self.r = {}
        self.wx = {}


class Multi:
    def __init__(self, members):
        self.members = list(members)


def flat(bufs):
    out = []
    for b in bufs:
        if isinstance(b, Multi):
            out.extend(b.members)
        else:
            out.append(b)
    return out


def run(g):
    for _ in g:
        pass


def merge(a, b):
    da = db = False
    while not (da and db):
        if not da:
            try:
                next(a)
            except StopIteration:
                da = True
        if not db:
            try:
                next(b)
            except StopIteration:
                db = True
        yield


def merge_all(gens):
    gens = [g for g in gens if g is not None]
    while gens:
        alive = []
        for g in gens:
            try:
                next(g)
                alive.append(g)
            except StopIteration:
                pass
        gens = alive
        yield


class Sched:
    CE = ("pe", "act", "dve", "pool")

    def __init__(self, nc, ctx):
        self.nc = nc
        self.eng = {"pe": nc.tensor, "act": nc.scalar, "dve": nc.vector, "pool": nc.gpsimd, "sp": nc.sync}
        self.sem = {e: ctx.enter_context(nc.semaphore("s_" + e)) for e in self.CE}
        self.cnt = {e: 0 for e in self.CE}
        self.ndq = 6
        self.dq = {}
        for q in ("sp", "pool"):
            for i in range(self.ndq):
                nm = "d%s%d" % (q, i)
                self.sem[nm] = ctx.enter_context(nc.semaphore(nm))
                self.cnt[nm] = 0
            self.dq[q] = 0
        self.seen = {e: {} for e in ("pe", "act", "dve", "pool", "sp")}
        self.snap = {}
        self.nwait = 0
        self.nins = 0

    def _wait(self, e, src, c):
        if c <= 0 or self.seen[e].get(src, 0) >= c:
            return
        self.eng[e].wait_ge(self.sem[src], c)
        self.nwait += 1
        self._absorb(e, src, c)

    def _absorb(self, e, src, c):
        se = self.seen[e]
        if se.get(src, 0) < c:
            se[src] = c
        sn = self.snap.get((src, c))
        if sn:
            for k, v in sn.items():
                if se.get(k, 0) < v:
                    se[k] = v

    def _deps(self, reads, writes):
        deps = {}
        for b in reads:
            if b.w is not None:
                s, c = b.w
                deps[s] = max(deps.get(s, 0), c)
            for s, c in b.wx.items():
                deps[s] = max(deps.get(s, 0), c)
        for b in writes:
            if b.w is not None:
                s, c = b.w
                deps[s] = max(deps.get(s, 0), c)
            for s, c in b.r.items():
                deps[s] = max(deps.get(s, 0), c)
        return deps

    def _mark(self, src, c, reads, writes):
        for b in reads:
            b.r[src] = c
        for b in writes:
            if src[0] == "d" and src not in self.CE:
                if b.w is not None and b.w[0] != src and b.w[0] not in self.CE:
                    b.wx[b.w[0]] = max(b.wx.get(b.w[0], 0), b.w[1])
                b.wx.pop(src, None)
            else:
                b.wx = {}
            b.w = (src, c)
            b.r = {}

    def op(self, e, fn, reads=(), writes=()):
        reads, writes = flat(reads), flat(writes)
        deps = self._deps(reads, writes)
        for s, c in deps.items():
            if s == e and e == "pe":
                continue
            self._wait(e, s, c)
        ins = fn()
        self.cnt[e] += 1
        ins.then_inc(self.sem[e], 1)
        self.snap[(e, self.cnt[e])] = dict(self.seen[e])
        self._mark(e, self.cnt[e], reads, writes)
        self.nins += 1
        return ins

    def dma(self, q, out, in_, reads=(), writes=(), parallel=False):
        reads, writes = flat(reads), flat(writes)
        deps = self._deps(reads, writes)
        if parallel:
            for b in writes:
                if b.w is not None and b.w[0] not in self.CE:
                    deps = self._deps(reads, [])
                    for b2 in writes:
                        if b2.w is not None and b2.w[0] in self.CE:
                            deps[b2.w[0]] = max(deps.get(b2.w[0], 0), b2.w[1])
                        for s_, c_ in b2.r.items():
                            deps[s_] = max(deps.get(s_, 0), c_)
                    break
        i = self.dq[q]
        self.dq[q] = (i + 1) % self.ndq
        nm = "d%s%d" % (q, i)
        deps[nm] = max(deps.get(nm, 0), self.cnt[nm])
        for s, c in deps.items():
            self._wait(q, s, c)
        ins = self.eng[q].dma_start(out=out, in_=in_)
        self.cnt[nm] += 16
        ins.then_inc(self.sem[nm], 16)
        self.snap[(nm, self.cnt[nm])] = dict(self.seen[q])
        self._mark(nm, self.cnt[nm], reads, writes)
        self.nins += 1
        return ins

    def barrier(self):
        for e in ("pe", "act", "dve", "pool", "sp"):
            for s, c in self.cnt.items():
                if s != e:
                    self._wait(e, s, c)


def make_consts():
    c = {}
    c["c_ident"] = np.eye(128, dtype=np.float32)
    blk = np.arange(128) // 64
    c["c_bones"] = (blk[:, None] == blk[None, :]).astype(np.float32)
    c["c_ones"] = np.ones((128, 128), np.float32)
    i = np.arange(128)
    for nm, bs in (("p", 128), ("s", 8)):
        same = (i[:, None] // bs) == (i[None, :] // bs)
        strict = ((i[None, :] > i[:, None]) & same).astype(np.float32)
        incl = ((i[None, :] >= i[:, None]) & same).astype(np.float32)
        c["c_mu_" + nm] = np.concatenate([strict, incl], axis=1)
        c["c_ml_" + nm] = ((i[None, :] < i[:, None]) & same).astype(np.float32)
    rm_p = np.ones((128, 512), np.float32)
    rm_p[:, ::128] = 0
    rm_s = np.ones((128, 128), np.float32)
    rm_s[:, ::8] = 0
    c["c_rm_p"] = rm_p
    c["c_rm_s"] = rm_s
    cm = np.zeros((16, 128), np.float32)
    for s in range(16):
        cm[s, s * 8:(s + 1) * 8] = 1
    c["c_colmask"] = np.broadcast_to(cm[None], (128, 16, 128)).copy()
    c["c_rowmask"] = cm.T.copy()
    hm = np.zeros((128, 2), np.float32)
    hm[:64, 0] = 1
    hm[64:, 1] = 1
    c["c_headmask"] = hm
    band = np.zeros((128, 4, 6, 128), np.float32)
    for g, w in enumerate((2, 4, 8, 16)):
        for t in range(128):
            for s in range(128):
                d = t - s
                if 0 <= d < w:
                    band[s, g, 0, t] += 1.0 / w
                    band[s, g, 2, t] += 1.0 / min(w, t + 1)
                    if s // 8 == t // 8:
                        band[s, g, 3, t] += 1.0 / w
                if s == t:
                    band[s, g, 0, t] -= 1.0
                    band[s, g, 2, t] -= 1.0
                    band[s, g, 3, t] -= 1.0
                if 0 <= t + 128 - s < w:
                    band[s, g, 1, t] += 1.0 / w
            seq, tl = t // 8, t % 8
            half, q = seq // 8, seq % 8
            for j in range(15):
                if j >= 15 + tl - w + 1:
                    band[q * 15 + j, g, 4 + half, t] += 1.0 / w
    c["c_band"] = band
    return c


def build():
    nc = bass.Bass("TRN2", target_bir_lowering=False)
    consts = make_consts()

    def din(name, shape):
        return nc.dram_tensor(name, list(shape), F32, kind="ExternalInput").ap()

    def dout(name, shape):
        return nc.dram_tensor(name, list(shape), F32, kind="ExternalOutput").ap()

    xp = din("xp", (2048, D))
    xs = din("xs", (128, D))
    sconv = din("sconv", (16, 30, 512))
    sshift = din("sshift", (16, 1792))
    swkv = din("swkv", (16, 8, 64, 64))
    spool = din("spool", (16, 15, D))
    w_in = din("w_in", (D, INP))
    conv_w = din("conv_w", (31, 512))
    vecs = din("vecs", (10, 512))
    mu = din("mu", (14, 128))
    w2a2 = din("w2a2", (128, 512))
    g2 = din("g2", (128, 512))
    w_out = din("w_out", (D, D))
    pool_w = din("pool_w", (4, 256, 256))
    pool_scale = din("pool_scale", (1, D))
    ln_mix_g = din("ln_mix_g", (2, D))
    ln_mix_b = din("ln_mix_b", (2, D))
    ffn_gate = din("ffn_gate", (2, D, DFF))
    ffn_up = din("ffn_up", (2, D, DFF))
    ffn_down = din("ffn_down", (2, DFF, D))
    ln_ffn_g = din("ln_ffn_g", (2, D))
    ln_ffn_b = din("ln_ffn_b", (2, D))
    cd = {k: din(k, v.shape) for k, v in consts.items()}

    y_p = dout("y_p", (2048, D))
    y_s = dout("y_s", (128, D))
    o_conv_p = dout("o_conv_p", (30, 512))
    o_shift_p = dout("o_shift_p", (1, 1792))
    o_wkv_p = dout("o_wkv_p", (8, 64, 64))
    o_pool_p = dout("o_pool_p", (15, D))
    o_conv_s = dout("o_conv_s", (16, 30, 512))
    o_shift_s = dout("o_shift_s", (16, 1792))
    o_wkv_s = dout("o_wkv_s", (16, 8, 64, 64))
    o_pool_s = dout("o_pool_s", (16, 15, D))
    outbufs = []

    with ExitStack() as ctx:
        S = Sched(nc, ctx)

        _names = {}

        def sbt(st, name, shape, dt):
            k = _names.get(name, 0)
            _names[name] = k + 1
            if k:
                name = "%s_%d" % (name, k)
            return st.enter_context(nc.sbuf_tensor(name, list(shape), dt))

        def V(fn, r=(), w=()):
            return S.op("dve", fn, r, w)

        def A(fn, r=(), w=()):
            return S.op("act", fn, r, w)

        def G(fn, r=(), w=()):
            return S.op("pool", fn, r, w)

        def T(fn, r=(), w=()):
            return S.op("pe", fn, r, w)

        def odma(out, in_, reads):
            b = Buf("out")
            S.dma("sp", out, in_, reads=reads, writes=[b])
            outbufs.append(b)

        PS = ctx.enter_context(nc.psum_tensor("PS", [128, 8, 512], F32))
        _pbk = [Buf("ps%d" % i) for i in range(8)]
        ph = [[_pbk[i], _pbk[i]] for i in range(8)]
        pb = [Multi([_pbk[i]]) for i in range(8)]
        x_all = sbt(ctx, "x_all", (128, NT, D), F32)
        bx = [Buf("x%d" % i) for i in range(NT)]
        ident = sbt(ctx, "ident", (128, 128), F32)
        identb = sbt(ctx, "identb", (128, 128), BF16)
        bonesf = sbt(ctx, "bonesf", (128, 128), F32)
        bonesb = sbt(ctx, "bonesb", (128, 128), BF16)
        lng = sbt(ctx, "lng", (128, D), F32)
        lnb = sbt(ctx, "lnb", (128, D), F32)
        lntmp = sbt(ctx, "lntmp", (128, D), F32)
        lnst = sbt(ctx, "lnst", (128, 2, 6), F32)
        lnmv = sbt(ctx, "lnmv", (128, 4), F32)
        b_c = Buf("consts")
        b_ln = Buf("lnconst")
        b_lnq = [Buf("lntmp%d" % i) for i in range(4)]
        b_lntmp = Multi(b_lnq)
        b_lnst = Buf("lnst")

        S.dma("sp", ident[:], cd["c_ident"], writes=[b_c])
        S.dma("sp", bonesf[:], cd["c_bones"], writes=[b_c])
        V(lambda: nc.vector.tensor_copy(identb[:], ident[:]), [b_c], [b_c])
        V(lambda: nc.vector.tensor_copy(bonesb[:], bonesf[:]), [b_c], [b_c])
        def load_x():
            for q in range(4):
                S.dma("sp", x_all[:, 4 * q:4 * q + 4, :],
                      xp[512 * q:512 * (q + 1), :].rearrange("(t p) d -> p t d", p=128), writes=bx[4 * q:4 * q + 4])
            S.dma("sp", x_all[:, 16, :], xs, writes=[bx[16]])

        def load_ln(gsrc, bsrc, i):
            S.dma("sp", lng[:], gsrc[i].partition_broadcast(128), writes=[b_ln])
            S.dma("sp", lnb[:], bsrc[i].partition_broadcast(128), writes=[b_ln])

        def layer_norm_tile(ti, mix_ap, mix_bufs, final_dst=None, gen=False):
            g = _ln_gen(ti, mix_ap, mix_bufs, final_dst)
            if gen:
                return g
            run(g)

        def _ln_gen(ti, mix_ap, mix_bufs, final_dst):
            xt = x_all[:, ti, :]
            if mix_ap is not None:
                V(lambda: nc.vector.scalar_tensor_tensor(lntmp[:], xt, ALPHA, mix_ap, ALU.mult, ALU.add),
                  [bx[ti]] + mix_bufs, [b_lntmp])
            V(lambda: nc.vector.bn_stats(lnst[:, 0, :], lntmp[:, 0:512]), [b_lntmp], [b_lnst])
            V(lambda: nc.vector.bn_stats(lnst[:, 1, :], lntmp[:, 512:1024]), [b_lntmp], [b_lnst])
            V(lambda: nc.vector.bn_aggr(lnmv[:, 0:2], lnst[:]), [b_lnst], [b_lnst])
            yield
            V(lambda: nc.vector.tensor_scalar(lnmv[:, 2:3], lnmv[:, 1:2], LN_EPS, None, ALU.add), [b_lnst], [b_lnst])
            G(lambda: nc.gpsimd.tensor_tensor(lnmv[:, 3:4], lnmv[:, 2:3], epsc[:, 2:3], ALU.pow), [b_lnst, b_c], [b_lnst])
            V(lambda: nc.vector.scalar_tensor_tensor(lnmv[:, 2:3], lnmv[:, 0:1], -1.0, lnmv[:, 3:4], ALU.mult, ALU.mult), [b_lnst], [b_lnst])
            A(lambda: nc.scalar.activation(lntmp[:], lntmp[:], AF.Identity, scale=lnmv[:, 3:4], bias=lnmv[:, 2:3]), [b_lntmp, b_lnst], [b_lntmp])
            yield
            V(lambda: nc.vector.tensor_tensor(lntmp[:], lntmp[:], lng[:], ALU.mult), [b_lntmp, b_ln], [b_lntmp])
            V(lambda: nc.vector.tensor_tensor(xt, lntmp[:], lnb[:], ALU.add), [b_lntmp, b_ln], [bx[ti]])
            if final_dst is not None:
                odma(final_dst, xt, [bx[ti]])
            yield

        epsc = sbt(ctx, "epsc", (128, 4), F32)
        V(lambda: nc.vector.memset(epsc[:, 0:1], LN_EPS), [], [b_c])
        V(lambda: nc.vector.memset(epsc[:, 1:2], GN_EPS), [], [b_c])
        V(lambda: nc.vector.memset(epsc[:, 2:3], -0.5), [], [b_c])

        def make_xT(xT, b_xT, tiles, ps_banks=(0, 1), col0=0):
            for j, ti in enumerate(tiles):
                for hf in range(2):
                    bk = ps_banks[hf]
                    for q in range(4):
                        kc = hf * 4 + q
                        T(lambda: nc.tensor.transpose(PS[:, bk, q * 128:(q + 1) * 128], x_all[:, ti, kc * 128:(kc + 1) * 128], ident[:]),
                          [bx[ti], b_c], [pb[bk]])
                    dst = xT[:, hf * 4:hf * 4 + 4, col0 + j * 128:col0 + (j + 1) * 128]
                    src = PS[:, bk, :].rearrange("p (q t) -> p q t", q=4)
                    if hf == 0:
                        V(lambda: nc.vector.tensor_copy(dst, src), [pb[bk]], [b_xT])
                    else:
                        A(lambda: nc.scalar.copy(dst, src), [pb[bk]], [b_xT])

        GROUPS = [([0, 1, 2, 3], 512), ([4, 5, 6, 7], 512), ([8, 9, 10, 11], 512), ([12, 13, 14, 15], 512), ([16], 128)]

        stopped = []
        try:
            with ExitStack() as l0:
                aT_all = sbt(l0, "aT_all", (128, 4, NT * 128), BF16)
                b_aT = Buf("aT")
                pv = sbt(l0, "pv", (128, 4, 10), F32)
                pmu = sbt(l0, "pmu", (128, 14), F32)
                omka = sbt(l0, "omka", (128, 4), F32)
                b_pv = Buf("pv")
                CB, CLG, CLB, W0, A0, KK, KA, RK, LXG, LXB = range(10)
                checkpoint(1)

                with ExitStack() as pa:
                    winA = sbt(pa, "winA", (128, 8, 1024), BF16)
                    onesf = sbt(pa, "onesf", (128, 128), F32)
                    S.dma("sp", onesf[:], cd["c_ones"], writes=[b_c])
                    b_winA = Buf("winA")
                    for kc in range(8):
                        S.dma("pool", winA[:, kc, :], w_in[kc * 128:(kc + 1) * 128, 0:1024], writes=[b_winA], parallel=True)
                    cw = sbt(pa, "cw", (128, 4, 31), F32)
                    diag = sbt(pa, "diag", (128, 4, 31, 128), BF16)
                    b_diag = Buf("diag")
                    cwst = sbt(pa, "cwst", (32, 512), F32)
                    S.dma("sp", cwst[0:31, :], conv_w, writes=[b_diag])
                    vst = sbt(pa, "vst", (16, 512), F32)
                    mst = sbt(pa, "mst", (16, 128), F32)
                    b_vst = Buf("vst")
                    S.dma("sp", vst[0:10, :], vecs, writes=[b_vst])
                    S.dma("sp", mst[0:14, :], mu, writes=[b_vst])
                    load_x()
                    for cc in range(4):
                        T(lambda: nc.tensor.transpose(PS[:, 1, cc * 10:cc * 10 + 10], vst[0:10, cc * 128:(cc + 1) * 128], ident[0:10, 0:10]),
                          [b_vst, b_c], [pb[1]])
                    T(lambda: nc.tensor.transpose(PS[:, 1, 64:78], mst[0:14, :], ident[0:14, 0:14]), [b_vst, b_c], [pb[1]])
                    V(lambda: nc.vector.tensor_copy(pv[:], PS[:, 1, 0:40].rearrange("p (c j) -> p c j", c=4)), [pb[1]], [b_pv])
                    V(lambda: nc.vector.tensor_copy(pmu[:], PS[:, 1, 64:78]), [pb[1]], [b_pv])
                    V(lambda: nc.vector.tensor_scalar(omka[:], pv[:, :, 6], -1.0, 1.0, ALU.mult, ALU.add), [b_pv], [b_pv])
                    for cc in range(4):
                        T(lambda: nc.tensor.transpose(PS[:, 0, cc * 31:cc * 31 + 31], cwst[0:31, cc * 128:(cc + 1) * 128], ident[0:31, 0:31]),
                          [b_diag, b_c], [pb[0]])
                    V(lambda: nc.vector.tensor_copy(cw[:], PS[:, 0, 0:124].rearrange("p (c j) -> p c j", c=4)), [pb[0]], [b_diag])
                    for cc in range(4):
                        V(lambda: nc.vector.tensor_tensor(diag[:, cc, :, :], ident[:].unsqueeze(1).to_broadcast([128, 31, 128]),
                                                          cw[:, cc, :].unsqueeze(2).to_broadcast([128, 31, 128]), ALU.mult),
                          [b_diag, b_c], [b_diag])
                    xT = sbt(pa, "xT_a", (128, 8, 512), BF16)
                    b_xT = Buf("xT")
                    uext = sbt(pa, "uext", (128, 4, 608), F32)
                    uxb = sbt(pa, "uxb", (128, 4, 608), BF16)
                    b_uext = [Buf("uext%d" % i) for i in range(4)]
                    b_uxb = [Buf("uxb%d" % i) for i in range(4)]
                    sgt = sbt(pa, "sgt", (128, 512), F32)
                    b_sgt = Buf("sgt")
                    cacc = sbt(pa, "cacc", (128, 4, 512), F32)
                    csq = sbt(pa, "csq", (128, 4, 512), F32)
                    b_cacc = [Buf("cacc%d" % i) for i in range(4)]
                    b_csq = [Buf("csq%d" % i) for i in range(4)]
                    st_m = sbt(pa, "st_m", (128, 512), F32)
                    st_r = sbt(pa, "st_r", (128, 512), F32)
                    st_t = sbt(pa, "st_t", (128, 512), F32)
                    b_st = Buf("st")
                    b_stt = Buf("stt")
                    ctm = sbt(pa, "ctm", (128, 4, 128), F32)
                    b_ctm = Buf("ctm")
                    for cc in range(4):
                        V(lambda: nc.vector.memset(uext[:, cc, 0:30], 0.0), [], [b_uext[cc]])

                    for gi, (tiles, ntok) in enumerate(GROUPS):
                        sample = (gi == 4)
                        t0 = tiles[0] * 128
                        make_xT(xT, b_xT, tiles)
                        if sample:
                            cst = sbt(pa, "cst", (128, 4, 512), F32)
                            b_cst = Buf("cst")
                            S.dma("sp", cst[0:120, :, :], sconv.rearrange("(a s) r c -> (s r) a c", a=4), writes=[b_cst])
                            for cc in range(4):
                                for a in range(4):
                                    T(lambda: nc.tensor.transpose(PS[:, 4 + cc, a * 120:(a + 1) * 120], cst[0:120, a, cc * 128:(cc + 1) * 128], ident[0:120, 0:120]),
                                      [b_cst, b_c], [pb[4 + cc]])
                                V(lambda: nc.vector.tensor_copy(
                                    uext[:, cc, 0:608].rearrange("p (q r) -> p q r", q=16)[:, :, 0:30],
                                    PS[:, 4 + cc, 0:480].rearrange("p (q r) -> p q r", q=16)), [pb[4 + cc]], [b_uext[cc]])
                        for cc in range(4):
                            for (bk, col0) in ((2, cc * 128), (3, 512 + cc * 128)):
                                for kc in range(8):
                                    T(lambda: nc.tensor.matmul(PS[:, bk, 0:ntok], winA[:, kc, col0:col0 + 128], xT[:, kc, 0:ntok],
                                                               start=(kc == 0), stop=(kc == 7)), [b_winA, b_xT], [pb[bk]])
                            A(lambda: nc.scalar.activation(sgt[:, 0:ntok], PS[:, 3, 0:ntok], AF.Sigmoid), [pb[3]], [b_sgt])
                            if not sample:
                                ud = uext[:, cc, 30:30 + ntok]
                                V(lambda: nc.vector.tensor_tensor(ud, PS[:, 2, 0:ntok], sgt[:, 0:ntok], ALU.mult), [pb[2], b_sgt], [b_uext[cc]])
                                ulen = 30 + ntok
                            else:
                                ud = uext[:, cc, 0:608].rearrange("p (q r) -> p q r", q=16)[:, :, 30:38]
                                V(lambda: nc.vector.tensor_tensor(ud, PS[:, 2, 0:128].rearrange("p (q r) -> p q r", q=16),
                                                                  sgt[:, 0:128].rearrange("p (q r) -> p q r", q=16), ALU.mult),
                                  [pb[2], b_sgt], [b_uext[cc]])
                                ulen = 608
                            A(lambda: nc.scalar.copy(uxb[:, cc, 0:ulen], uext[:, cc, 0:ulen]), [b_uext[cc]], [b_uxb[cc]])
                            for j in range(31):
                                if not sample:
                                    rhs = uxb[:, cc, j:j + ntok]
                                else:
                                    rhs = uxb[:, cc, 0:608].rearrange("p (q r) -> p q r", q=16)[:, :, j:j + 8]
                                T(lambda: nc.tensor.matmul(PS[:, 4, 0:ntok], diag[:, cc, j, :], rhs, start=(j == 0), stop=(j == 30)),
                                  [b_diag, b_uxb[cc]], [pb[4]])
                            A(lambda: nc.scalar.activation(cacc[:, cc, 0:ntok], PS[:, 4, 0:ntok], AF.Identity, bias=pv[:, cc, CB:CB + 1]),
                              [pb[4], b_pv], [b_cacc[cc]])
                            V(lambda: nc.vector.tensor_tensor(csq[:, cc, 0:ntok], cacc[:, cc, 0:ntok], cacc[:, cc, 0:ntok], ALU.mult),
                              [b_cacc[cc]], [b_csq[cc]])
                        for cc in range(4):
                            T(lambda: nc.tensor.matmul(PS[:, 5, 0:ntok], onesf[:], cacc[:, cc, 0:ntok], start=(cc == 0), stop=(cc == 3)),
                              [b_c, b_cacc[cc]], [pb[5]])
                        for cc in range(4):
                            T(lambda: nc.tensor.matmul(PS[:, 6, 0:ntok], onesf[:], csq[:, cc, 0:ntok], start=(cc == 0), stop=(cc == 3)),
                              [b_c, b_csq[cc]], [pb[6]])
                        A(lambda: nc.scalar.mul(st_m[:, 0:ntok], PS[:, 5, 0:ntok], 1.0 / 512), [pb[5]], [b_st])
                        V(lambda: nc.vector.tensor_tensor(st_t[:, 0:ntok], st_m[:, 0:ntok], st_m[:, 0:ntok], ALU.mult), [b_st], [b_stt])
                        V(lambda: nc.vector.scalar_tensor_tensor(st_t[:, 0:ntok], PS[:, 6, 0:ntok], 1.0 / 512, st_t[:, 0:ntok], ALU.mult, ALU.subtract),
                          [pb[6], b_stt], [b_stt])
                        A(lambda: nc.scalar.activation(st_t[:, 0:ntok], st_t[:, 0:ntok], AF.Ln, bias=epsc[:, 0:1]), [b_stt, b_c], [b_stt])
                        A(lambda: nc.scalar.activation(st_r[:, 0:ntok], st_t[:, 0:ntok], AF.Exp, scale=-0.5), [b_stt], [b_st])
                        for cc in range(4):
                            V(lambda: nc.vector.tensor_tensor(csq[:, cc, 0:ntok], cacc[:, cc, 0:ntok], st_m[:, 0:ntok], ALU.subtract),
                              [b_cacc[cc], b_st], [b_csq[cc]])
                            V(lambda: nc.vector.tensor_tensor(csq[:, cc, 0:ntok], csq[:, cc, 0:ntok], st_r[:, 0:ntok], ALU.mult),
                              [b_csq[cc], b_st], [b_csq[cc]])
                            A(lambda: nc.scalar.activation(aT_all[:, cc, t0:t0 + ntok], csq[:, cc, 0:ntok], AF.Silu,
                                                           bias=pv[:, cc, CLB:CLB + 1], scale=pv[:, cc, CLG:CLG + 1]),
                              [b_csq[cc], b_pv], [b_aT])
                        if not sample:
                            if gi < 3:
                                for cc in range(4):
                                    V(lambda: nc.vector.tensor_copy(uext[:, cc, 0:30], uext[:, cc, ntok:ntok + 30]), [b_uext[cc]], [b_uext[cc]])
                            else:
                                for cc in range(4):
                                    T(lambda: nc.tensor.transpose(PS[0:30, 7, cc * 128:(cc + 1) * 128], uext[:, cc, ntok:ntok + 30], ident[:]),
                                      [b_uext[cc], b_c], [pb[7]])
                                V(lambda: nc.vector.tensor_copy(ctm[0:30, :, :], PS[0:30, 7, :].rearrange("p (c j) -> p c j", c=4)), [pb[7]], [b_ctm])
                                odma(o_conv_p, ctm[0:30, :, :].rearrange("p c j -> p (c j)"), [b_ctm])
                                S.barrier()
                        else:
                            for cc in range(4):
                                V(lambda: nc.vector.tensor_copy(ctm[:, cc, :].rearrange("p (q r) -> p q r", q=16),
                                                                uext[:, cc, 0:608].rearrange("p (q r) -> p q r", q=16)[:, :, 30:38]),
                                  [b_uext[cc]], [b_ctm])
                            for cc in range(4):
                                T(lambda: nc.tensor.transpose(PS[:, 7, cc * 128:(cc + 1) * 128], ctm[:, cc, :], ident[:]), [b_ctm, b_c], [pb[7]])
                            V(lambda: nc.vector.tensor_copy(sgt[:, :], PS[:, 7, :]), [pb[7]], [b_sgt])
                            odma(o_conv_s[:, 22:30, :], sgt[:, :], [b_sgt])
                            odma(o_conv_s[:, 0:22, :], sconv[:, 8:30, :], [])
                    S.barrier()

                checkpoint(2)
                with ExitStack() as pbk:
                    winB = sbt(pbk, "winB", (128, 8, 1792), BF16)
                    woutb = sbt(pbk, "woutb", (128, 8, 1024), BF16)
                    lw2 = sbt(pbk, "lw2", (128, 512), BF16)
                    lg2 = sbt(pbk, "lg2", (128, 512), BF16)
                    b_wB = Buf("wB")
                    b_wBc = [Buf("wBc%d" % i) for i in range(14)]

                    def load_wchunk(ci):
                        S.dma("pool", winB[:, :, ci * 128:(ci + 1) * 128],
                              w_in[:, 1024 + ci * 128:1024 + (ci + 1) * 128].rearrange("(k p) n -> p k n", p=128), writes=[b_wBc[ci]])

                    load_wchunk(12)
                    load_wchunk(13)
                    S.dma("pool", lw2[:], w2a2, writes=[b_wB])
                    S.dma("pool", lg2[:], g2, writes=[b_wB])
                    for cc_ in range(4):
                        for base_ in (0, 4, 8):
                            load_wchunk(base_ + cc_)
                    for kc in range(8):
                        S.dma("pool", woutb[:, kc, :], w_out[kc * 128:(kc + 1) * 128, :], writes=[b_wB], parallel=True)
                    load_ln(ln_mix_g, ln_mix_b, 0)
                    mu_t = {}
                    msk = {}
                    rm_s = sbt(pbk, "rm_s", (128, 128), F32)
                    headmask = sbt(pbk, "headmask", (128, 2), F32)
                    S.dma("sp", rm_s[:], cd["c_rm_s"], writes=[b_c])
                    S.dma("sp", headmask[:], cd["c_headmask"], writes=[b_c])

                    b_H = [Buf("H%d" % i) for i in range(4)]
                    b_Hs = [Buf("Hs%d" % i) for i in range(4)]
                    carry = sbt(pbk, "carry", (128, 14), F32)
                    lend = sbt(pbk, "lend", (128, 2), F32)
                    b_lend = Buf("lend")
                    omm = sbt(pbk, "omm", (128, 14), F32)
                    V(lambda: nc.vector.tensor_scalar(omm[:], pmu[:], -1.0, 1.0, ALU.mult, ALU.add), [b_pv], [b_pv])
                    b_cm = Buf("masks")
                    b_carry = [Buf("carry%d" % i) for i in range(14)]
                    V(lambda: nc.vector.memset(carry[:], 0.0), [], b_carry)

                    xT = sbt(pbk, "xT_b", (128, 8, 256), BF16)
                    b_xT = Buf("xTb")
                    pB = sbt(pbk, "pB", (128, 3, 1 + 256), F32)
                    xm = sbt(pbk, "xm", (128, 3, 256), F32)
                    b_pB = [Buf("pB%d" % i) for i in range(3)]
                    b_xm = [Buf("xm%d" % i) for i in range(3)]
                    lwa = sbt(pbk, "lwa", (128, 256), BF16)
                    sgd = sbt(pbk, "sgd", (128, 256), BF16)
                    b_lwa = Buf("lwa")
                    b_sgd = Buf("sgd")
                    names = ["lw", "Lc", "aa", "kkn", "kap", "E0", "Em", "Ee", "t1", "t2"]
                    ft = {n: sbt(pbk, "f_" + n, (128, 256), F32) for n in names}
                    fb = {n: Buf("f_" + n) for n in names}
                    Bh = sbt(pbk, "Bh", (128, 256), BF16)
                    Kh = sbt(pbk, "Kh", (128, 256), BF16)
                    xvb = sbt(pbk, "xvb", (128, 256), BF16)
                    sqb = sbt(pbk, "sqb", (128, 256), BF16)
                    b_Bh, b_Kh, b_xvb, b_sqb = [Buf(n) for n in ("Bh", "Kh", "xvb", "sqb")]
                    oAR, oARm, oBt, oBtm, oKt, oTM3, oE1, obon, ogt = [[] for _ in range(9)]
                    b_oAR, b_oARm, b_oBt, b_oBtm, b_oKt, b_oTM3, b_oE1, b_obon, b_ogt = [[] for _ in range(9)]

                    def alloc_set(st):
                        i = len(oAR)
                        oAR.append(sbt(st, "oAR", (128, 2, 2, 128), BF16))
                        oARm.append(sbt(st, "oARm", (128, 2, 2, 2, 128), BF16))
                        oBt.append(sbt(st, "oBt", (128, 256), BF16))
                        oBtm.append(sbt(st, "oBtm", (128, 2, 256), BF16))
                        oKt.append(sbt(st, "oKt", (128, 256), BF16))
                        oTM3.append(sbt(st, "oTM3", (128, 2, 384), BF16))
                        oE1.append(sbt(st, "oE1", (128, 256), F32))
                        obon.append(sbt(st, "obon", (128, 256), F32))
                        ogt.append(sbt(st, "ogt", (128, 256), F32))
                        for lst, nm in ((b_oAR, "AR"), (b_oARm, "ARm"), (b_oBt, "Bt"), (b_oBtm, "Btm"), (b_oKt, "Kt"), (b_oTM3, "TM3"),
                                        (b_oE1, "E1"), (b_obon, "bon"), (b_ogt, "gt")):
                            lst.append(Buf("%s%d" % (nm, i)))

                    cLM1, cLM2, cL0, cNL, cQ = [], [], [], [], []
                    b_cLM1, b_cLM2, b_cL0, b_cNL, b_cQ = [], [], [], [], []

                    def alloc_chain(st):
                        i = len(cLM1)
                        cLM1.append(sbt(st, "cLM1", (128, 2, 256), BF16))
                        cLM2.append(sbt(st, "cLM2", (128, 2, 256), BF16))
                        cL0.append(sbt(st, "cL0", (128, 2, 128), BF16))
                        cNL.append([sbt(st, "cNL", (128, 2, 2, 128), BF16) for _ in range(2)])
                        cQ.append([sbt(st, "cQ", (128, 2, 128), BF16) for _ in range(2)])
                        b_cLM1.append(Buf("cLM1_%d" % i))
                        b_cLM2.append(Buf("cLM2_%d" % i))
                        b_cL0.append(Buf("cL0_%d" % i))
                        b_cNL.append([Buf("cNL%d_%d" % (i, k)) for k in range(2)])
                        b_cQ.append([Buf("cQ%d_%d" % (i, k)) for k in range(2)])

                    alloc_set(pbk)
                    alloc_chain(pbk)
                    gnt = [ft["t1"], ft["t2"], ft["E0"]]
                    b_gnt = [fb["t1"], fb["t2"], fb["E0"]]
                    Xs = sbt(pbk, "Xs", (128, 2, 64), BF16)
                    Ub = sbt(pbk, "Ub", (128, 2, 64), BF16)
                    b_Xs, b_Ub = Buf("Xs"), Buf("Ub")
                    bT = sbt(pbk, "bT", (128, 4, 256), BF16)
                    b_bT = Buf("bT")

                    smp = ExitStack()
                    mu_t["s"] = sbt(smp, "mu_s", (128, 256), F32)
                    msk["s"] = sbt(smp, "ml_s", (128, 128), F32)
                    S.dma("sp", mu_t["s"][:], cd["c_mu_s"], writes=[b_cm])
                    S.dma("sp", msk["s"][:], cd["c_ml_s"], writes=[b_cm])
                    colmask = sbt(smp, "colmask", (128, 16, 128), BF16)
                    rowmask = sbt(smp, "rowmask", (128, 16), F32)
                    S.dma("pool", colmask[:], cd["c_colmask"], writes=[b_c])
                    S.dma("sp", rowmask[:], cd["c_rowmask"], writes=[b_c])
                    Hsf = sbt(smp, "Hsf", (128, 16, 64), F32)
                    Hsb = sbt(smp, "Hsb", (128, 16, 64), BF16)
                    scarry = sbt(smp, "scarry", (128, 14, 16), F32)
                    smask = sbt(smp, "smask", (128, 8, 128), BF16)
                    b_smask = Buf("smask")
                    wst = sbt(smp, "wst", (64, 8, 128), F32)
                    b_wst = Buf("wst")
                    stg = sbt(smp, "stg", (16, 512), F32)
                    b_stg = Buf("stg")
                    for q in range(4):
                        w_ = 512 if q < 3 else 256
                        S.dma("sp", stg[0:16, 0:w_], sshift[:, q * 512:q * 512 + w_], writes=[b_stg])
                        for r in range(w_ // 128):
                            ci = q * 4 + r
                            T(lambda: nc.tensor.transpose(PS[:, 0, ci * 16:ci * 16 + 16], stg[0:16, r * 128:(r + 1) * 128], ident[0:16, 0:16]),
                              [b_stg, b_c], [pb[0]])
                    V(lambda: nc.vector.tensor_copy(scarry[:], PS[:, 0, 0:224].rearrange("p (c s) -> p c s", c=14)), [pb[0]], b_carry)

                    def load_sample_state(cc):
                        for half in range(2):
                            bk = 5 + half
                            for h2 in range(2):
                                S.dma("sp", wst[:, :, h2 * 64:(h2 + 1) * 64], swkv[half * 8:(half + 1) * 8, 2 * cc + h2, :, :].rearrange("s v k -> v s k"), writes=[b_wst])
                            for q in range(8):
                                T(lambda: nc.tensor.transpose(PS[:, bk, q * 64:(q + 1) * 64], wst[:, q, :], ident[0:64, 0:64]), [b_wst, b_c], [pb[bk]])
                            V(lambda: nc.vector.tensor_copy(Hsf[:, half * 8:(half + 1) * 8, :], PS[:, bk, :].rearrange("p (q v) -> p q v", q=8)),
                              [pb[bk]], [b_Hs[cc]])
                        A(lambda: nc.scalar.copy(Hsb[:], Hsf[:]), [b_Hs[cc]], [b_Hs[cc]])

                    def store_sample_state(cc):
                        for half in range(2):
                            for q2 in range(2):
                                bk = 5 + q2
                                for r in range(4):
                                    s_ = half * 8 + q2 * 4 + r
                                    T(lambda: nc.tensor.transpose(PS[0:64, bk, r * 128:(r + 1) * 128], Hsf[:, s_, :], ident[:]), [b_Hs[cc], b_c], [pb[bk]])
                                V(lambda: nc.vector.tensor_copy(wst[:, q2 * 4:(q2 + 1) * 4, :], PS[0:64, bk, :].rearrange("p (r x) -> p r x", r=4)), [pb[bk]], [b_wst])
                            for h2 in range(2):
                                odma(o_wkv_s[half * 8:(half + 1) * 8, 2 * cc + h2, :, :].rearrange("s v k -> v s k"), wst[:, :, h2 * 64:(h2 + 1) * 64], [b_wst])

                    def shifted_chunk(slot, ci, bank, ntok, sample, off=0, pbuf=None):
                        pbuf = pb[bank] if pbuf is None else pbuf
                        if not sample:
                            A(lambda: nc.scalar.copy(pB[:, slot, 1:1 + ntok], PS[:, bank, off:off + ntok]), [pbuf], [b_pB[slot]])
                            V(lambda: nc.vector.tensor_copy(pB[:, slot, 0:1], carry[:, ci:ci + 1]), [b_carry[ci]], [b_pB[slot]])
                            V(lambda: nc.vector.tensor_copy(carry[:, ci:ci + 1], pB[:, slot, ntok:ntok + 1]), [b_pB[slot]], [b_carry[ci]])
                            cur = pB[:, slot, 1:1 + ntok]
                            prv = pB[:, slot, 0:ntok]
                            dst = xm[:, slot, 0:ntok]
                            psrc = PS[:, bank, off:off + ntok]
                        else:
                            v3 = pB[:, slot, 0:144].rearrange("p (q r) -> p q r", q=16)
                            A(lambda: nc.scalar.copy(v3[:, :, 1:9], PS[:, bank, off:off + 128].rearrange("p (q r) -> p q r", q=16)), [pbuf], [b_pB[slot]])
                            V(lambda: nc.vector.tensor_copy(v3[:, :, 0], scarry[:, ci, :]), [b_carry[ci]], [b_pB[slot]])
                            V(lambda: nc.vector.tensor_copy(scarry[:, ci, :], v3[:, :, 8]), [b_pB[slot]], [b_carry[ci]])
                            cur = v3[:, :, 1:9]
                            prv = v3[:, :, 0:8]
                            dst = xm[:, slot, 0:128].rearrange("p (q r) -> p q r", q=16)
                            psrc = PS[:, bank, off:off + 128].rearrange("p (q r) -> p q r", q=16)
                        A(lambda: nc.scalar.activation(dst, psrc, AF.Identity, scale=omm[:, ci:ci + 1]), [pbuf, b_pv], [b_xm[slot]])
                        V(lambda: nc.vector.scalar_tensor_tensor(dst, prv, pmu[:, ci:ci + 1], dst, ALU.mult, ALU.add),
                          [b_xm[slot], b_pB[slot], b_pv], [b_xm[slot]])

                    def proj_chunk(ci, bank, ntok, off=0, pbuf=None):
                        pbuf = pb[bank] if pbuf is None else pbuf
                        for kc in range(8):
                            T(lambda: nc.tensor.matmul(PS[:, bank, off:off + ntok], winB[:, kc, ci * 128:(ci + 1) * 128], xT[:, kc, 0:ntok],
                                                       start=(kc == 0), stop=(kc == 7)), [b_wBc[ci], b_xT], [pbuf])

                    GROUPS_B = [([16], 128)] + [([2 * i, 2 * i + 1], 256) for i in range(8)]
                    Hf = Hb = rm_p = None
                    def sample_to_prompt_transition():
                        nonlocal Hf, Hb, rm_p, gnt, b_gnt
                        for ci in range(14):
                            T(lambda: nc.tensor.transpose(PS[0:16, 4 + ci // 4, (ci % 4) * 128:(ci % 4 + 1) * 128], scarry[:, ci, :], ident[:]),
                              [b_carry[ci], b_c], [pb[4 + ci // 4]])
                        for q in range(4):
                            w_ = 512 if q < 3 else 256
                            V(lambda: nc.vector.tensor_copy(stg[0:16, 0:w_], PS[0:16, 4 + q, 0:w_]), [pb[4 + q]], [b_stg])
                            odma(o_shift_s[:, q * 512:q * 512 + w_], stg[0:16, 0:w_], [b_stg])
                        checkpoint(3) if q == 3 else None
                        S.barrier()
                        smp.close()
                        Hf = sbt(pbk, "Hf", (128, 4, 64), F32)
                        Hb = sbt(pbk, "Hb", (128, 4, 64), BF16)
                        V(lambda: nc.vector.memset(Hf[:], 0.0), [], b_H)
                        V(lambda: nc.vector.memset(Hb[:], 0.0), [], b_H)
                        mu_t["p"] = sbt(pbk, "mu_p", (128, 256), F32)
                        msk["p"] = sbt(pbk, "ml_p", (128, 128), F32)
                        rm_p = sbt(pbk, "rm_p", (128, 256), F32)
                        alloc_set(pbk)
                        alloc_chain(pbk)
                        gnt = [lntmp[:, 256 * i:256 * (i + 1)] for i in range(3)]
                        b_gnt = b_lnq[0:3]
                        print("sbuf bytes remaining in prompt scope:", nc.sbuf_bytes_remaining)
                        S.dma("sp", mu_t["p"][:], cd["c_mu_p"], writes=[b_cm])
                        S.dma("sp", msk["p"][:], cd["c_ml_p"], writes=[b_cm])
                        S.dma("sp", rm_p[:], cd["c_rm_p"][:, 0:256], writes=[b_cm])
                    def group_fns(gi, tiles, ntok):
                        sample = (gi == 0)
                        nt = len(tiles)
                        mk = "s" if sample else "p"
                        def gen_pre():
                            for j_, ti_ in enumerate(tiles):
                                make_xT(xT, b_xT, [ti_], col0=j_ * 128, ps_banks=(6, 7))
                                yield
                            proj_chunk(12, 6, ntok)
                            shifted_chunk(0, 12, 6, ntok, sample)
                            A(lambda: nc.scalar.activation(lwa[0:64, 0:ntok], xm[0:64, 0, 0:ntok], AF.Tanh), [b_xm[0]], [b_lwa])
                            A(lambda: nc.scalar.copy(lwa[64:128, 0:ntok], xm[64:128, 0, 0:ntok]), [b_xm[0]], [b_lwa])
                            yield
                            proj_chunk(13, 7, ntok)
                            shifted_chunk(1, 13, 7, ntok, sample)
                            A(lambda: nc.scalar.activation(sgd[:, 0:ntok], xm[:, 1, 0:ntok], AF.Sigmoid), [b_xm[1]], [b_sgd])
                            yield
                            if not sample:
                                yield from gen_prep(0, 0)

                        def gen_prep(cc, bs):
                            f = {n: ft[n][:, 0:ntok] for n in names}
                            E1 = oE1[bs][:, 0:ntok]
                            bon = obon[bs][:, 0:ntok]
                            gt = ogt[bs][:, 0:ntok]
                            if sample:
                                load_sample_state(cc)
                            slots = ((6, 0, ph[6][0]), (6, ntok, ph[6][1]), (7, 0, ph[7][0]))
                            for slot, base in enumerate((0, 4, 8)):
                                bk_, off, pbf = slots[slot]
                                proj_chunk(base + cc, bk_, ntok, off, pbf)
                                shifted_chunk(slot, base + cc, bk_, ntok, sample, off, pbf)
                                yield
                            xr, xk, xv = xm[:, 0, 0:ntok], xm[:, 1, 0:ntok], xm[:, 2, 0:ntok]
                            pw_ = PS[:, 7, ntok:2 * ntok]
                            T(lambda: nc.tensor.matmul(pw_, lw2[0:64, cc * 128:(cc + 1) * 128], lwa[0:64, 0:ntok], start=True, stop=True),
                              [b_wB, b_lwa], [ph[7][1]])
                            A(lambda: nc.scalar.activation(f["lw"], pw_, AF.Sigmoid, bias=pv[:, cc, W0:W0 + 1]), [ph[7][1], b_pv], [fb["lw"]])
                            pa_ = PS[:, 6, 0:ntok]
                            T(lambda: nc.tensor.matmul(pa_, lw2[64:128, cc * 128:(cc + 1) * 128], lwa[64:128, 0:ntok], start=True, stop=True),
                              [b_wB, b_lwa], [ph[6][0]])
                            A(lambda: nc.scalar.activation(f["aa"], pa_, AF.Sigmoid, bias=pv[:, cc, A0:A0 + 1]), [ph[6][0], b_pv], [fb["aa"]])
                            yield
                            pg_ = PS[:, 6, ntok:2 * ntok]
                            T(lambda: nc.tensor.matmul(pg_, lg2[:, cc * 128:(cc + 1) * 128], sgd[:, 0:ntok], start=True, stop=True),
                              [b_wB, b_sgd], [ph[6][1]])
                            A(lambda: nc.scalar.copy(gt, pg_), [ph[6][1]], [b_ogt[bs]])
                            rm = rm_s[:, 0:128] if sample else rm_p[:, 0:ntok]
                            V(lambda: nc.vector.tensor_tensor_scan(f["Lc"], rm, f["lw"], 0.0, ALU.mult, ALU.add), [fb["lw"], b_c, b_cm], [fb["Lc"]])
                            yield
                            A(lambda: nc.scalar.activation(E1, f["Lc"], AF.Exp, scale=DECAY_C), [fb["Lc"]], [b_oE1[bs]])
                            A(lambda: nc.scalar.activation(f["Em"], f["Lc"], AF.Exp, scale=-DECAY_C), [fb["Lc"]], [fb["Em"]])
                            V(lambda: nc.vector.tensor_tensor(f["t1"], f["Lc"], f["lw"], ALU.subtract), [fb["Lc"], fb["lw"]], [fb["t1"]])
                            A(lambda: nc.scalar.activation(f["E0"], f["t1"], AF.Exp, scale=DECAY_C), [fb["t1"]], [fb["E0"]])
                            yield
                            if not sample:
                                for j in range(nt):
                                    sl = slice(j * 128, (j + 1) * 128)
                                    V(lambda: nc.vector.tensor_scalar(lend[:, j:j + 1], ft["Lc"][:, j * 128 + 127:j * 128 + 128], DECAY_C, None, ALU.mult),
                                      [fb["Lc"]], [b_lend])
                                    A(lambda: nc.scalar.activation(ft["Ee"][:, sl], ft["Lc"][:, sl], AF.Exp, scale=-DECAY_C,
                                                                   bias=lend[:, j:j + 1]), [fb["Lc"], b_lend], [fb["Ee"]])
                            else:
                                L3 = ft["Lc"][:, 0:128].rearrange("p (q r) -> p q r", q=16)
                                V(lambda: nc.vector.tensor_tensor(ft["t2"][:, 0:128].rearrange("p (q r) -> p q r", q=16),
                                                                  L3[:, :, 7:8].to_broadcast([128, 16, 8]), L3, ALU.subtract), [fb["Lc"]], [fb["t2"]])
                                A(lambda: nc.scalar.activation(f["Ee"], f["t2"], AF.Exp, scale=DECAY_C), [fb["t2"]], [fb["Ee"]])
                            A(lambda: nc.scalar.activation(f["kkn"], xk, AF.Identity, scale=pv[:, cc, KK:KK + 1]), [b_xm[1], b_pv], [fb["kkn"]])
                            V(lambda: nc.vector.tensor_tensor(sqb[:, 0:ntok], f["kkn"], f["kkn"], ALU.mult), [fb["kkn"]], [b_sqb])
                            yield
                            pss = PS[:, 7, 0:ntok]
                            T(lambda: nc.tensor.matmul(pss, bonesb[:], sqb[:, 0:ntok], start=True, stop=True), [b_c, b_sqb], [ph[7][0]])
                            V(lambda: nc.vector.tensor_scalar(f["t1"], pss, 1e-24, None, ALU.max), [ph[7][0]], [fb["t1"]])
                            A(lambda: nc.scalar.activation(f["t1"], f["t1"], AF.Ln), [fb["t1"]], [fb["t1"]])
                            A(lambda: nc.scalar.activation(f["t1"], f["t1"], AF.Exp, scale=-0.5), [fb["t1"]], [fb["t1"]])
                            yield
                            V(lambda: nc.vector.tensor_tensor(f["kkn"], f["kkn"], f["t1"], ALU.mult), [fb["kkn"], fb["t1"]], [fb["kkn"]])
                            A(lambda: nc.scalar.activation(f["t2"], f["aa"], AF.Identity, scale=pv[:, cc, KA:KA + 1], bias=omka[:, cc:cc + 1]),
                              [fb["aa"], b_pv], [fb["t2"]])
                            V(lambda: nc.vector.tensor_tensor(f["kap"], xk, f["t2"], ALU.mult), [b_xm[1], fb["t2"]], [fb["kap"]])
                            yield
                            AR4 = oAR[bs][:, 0:nt, :, :]
                            v3 = lambda ap: ap.rearrange("p (i t) -> p i t", i=nt)
                            V(lambda: nc.vector.scalar_tensor_tensor(AR4[:, :, 0, :], v3(f["kkn"]), -1.0, v3(f["E0"]), ALU.mult, ALU.mult),
                              [fb["kkn"], fb["E0"]], [b_oAR[bs]])
                            V(lambda: nc.vector.tensor_tensor(AR4[:, :, 1, :], v3(xr), v3(E1), ALU.mult), [b_xm[0], b_oE1[bs]], [b_oAR[bs]])
                            V(lambda: nc.vector.tensor_tensor(f["t1"], f["kkn"], f["aa"], ALU.mult), [fb["kkn"], fb["aa"]], [fb["t1"]])
                            yield
                            V(lambda: nc.vector.tensor_tensor(oBt[bs][:, 0:ntok], f["t1"], f["Em"], ALU.mult), [fb["t1"], fb["Em"]], [b_oBt[bs]])
                            V(lambda: nc.vector.tensor_tensor(Bh[:, 0:ntok], f["t1"], f["Ee"], ALU.mult), [fb["t1"], fb["Ee"]], [b_Bh])
                            V(lambda: nc.vector.tensor_tensor(oKt[bs][:, 0:ntok], f["kap"], f["Em"], ALU.mult), [fb["kap"], fb["Em"]], [b_oKt[bs]])
                            V(lambda: nc.vector.tensor_tensor(Kh[:, 0:ntok], f["kap"], f["Ee"], ALU.mult), [fb["kap"], fb["Ee"]], [b_Kh])
                            yield
                            for h2 in range(2):
                                A(lambda: nc.scalar.activation(oARm[bs][:, h2, 0:nt, :, :], AR4, AF.Identity, scale=headmask[:, h2:h2 + 1]), [b_oAR[bs], b_c], [b_oARm[bs]])
                                A(lambda: nc.scalar.activation(oBtm[bs][:, h2, 0:ntok], oBt[bs][:, 0:ntok], AF.Identity, scale=headmask[:, h2:h2 + 1]), [b_oBt[bs], b_c], [b_oBtm[bs]])
                            A(lambda: nc.scalar.copy(xvb[:, 0:ntok], xv), [b_xm[2]], [b_xvb])
                            yield
                            V(lambda: nc.vector.scalar_tensor_tensor(sqb[:, 0:ntok], xr, pv[:, cc, RK:RK + 1], f["kap"], ALU.mult, ALU.mult),
                              [b_xm[0], fb["kap"], b_pv], [b_sqb])
                            prk = PS[:, 7, ntok:2 * ntok]
                            T(lambda: nc.tensor.matmul(prk, bonesb[:], sqb[:, 0:ntok], start=True, stop=True), [b_c, b_sqb], [ph[7][1]])
                            V(lambda: nc.vector.tensor_tensor(bon, prk, xv, ALU.mult), [ph[7][1], b_xm[2]], [b_obon[bs]])
                            yield
                            pst = PS[:, 6, :].bitcast(BF16)
                            tmb = lambda j: pb[6] if sample else ph[6][j]
                            for j in range(nt):
                                sl = slice(j * 128, (j + 1) * 128)
                                for q, (src, bsrc) in enumerate(((xvb, b_xvb), (Bh, b_Bh), (Kh, b_Kh))):
                                    T(lambda: nc.tensor.transpose(pst[:, j * 512 + q * 128:j * 512 + (q + 1) * 128], src[:, sl], identb[:]), [bsrc, b_c], [tmb(j)])
                                V(lambda: nc.vector.tensor_copy(oTM3[bs][:, j, :], pst[:, j * 512:j * 512 + 384]), [tmb(j)], [b_oTM3[bs]])
                                yield

                        def gen_mats(cc, bs, j):
                            sl = slice(j * 128, (j + 1) * 128)
                            rhsAR = oARm[bs][:, :, j, :, :].rearrange("p h a t -> p h (a t)")
                            T(lambda: nc.tensor.matmul(PS[:, 2, :], oKt[bs][:, sl], rhsAR, start=True, stop=True), [b_oKt[bs], b_oARm[bs]], [pb[2]])
                            T(lambda: nc.tensor.matmul(PS[:, 5, :], oBt[bs][:, sl], rhsAR, start=True, stop=True), [b_oBt[bs], b_oARm[bs]], [pb[5]])
                            T(lambda: nc.tensor.matmul(PS[:, 4, 0:256], oAR[bs][:, j, 0, :], oBtm[bs][:, :, sl], start=True, stop=True), [b_oAR[bs], b_oBtm[bs]], [ph[4][0]])
                            mub = mu_t[mk][:].unsqueeze(1).to_broadcast([128, 2, 256])
                            V(lambda: nc.vector.tensor_tensor(cLM1[j][:], PS[:, 2, :].rearrange("p (h x) -> p h x", h=2), mub, ALU.mult), [pb[2], b_cm], [b_cLM1[j]])
                            V(lambda: nc.vector.tensor_tensor(cLM2[j][:], PS[:, 5, :].rearrange("p (h x) -> p h x", h=2), mub, ALU.mult), [pb[5], b_cm], [b_cLM2[j]])
                            V(lambda: nc.vector.tensor_tensor(cL0[j][:], PS[:, 4, 0:256].rearrange("p (h x) -> p h x", h=2),
                                                              msk[mk][:].unsqueeze(1).to_broadcast([128, 2, 128]), ALU.mult), [ph[4][0], b_cm], [b_cL0[j]])
                            yield

                        def gen_chain(cc, bs, j, res):
                            if j == 0:
                                nlb, nlbuf, qap, qbuf = 5, pb[5], PS[:, 4, 256:512], ph[4][1]
                            else:
                                nlb, nlbuf, qap, qbuf = 2, pb[2], PS[:, 1, 256:512], ph[1][1]
                            Np = lambda h2: cLM2[j][:, h2, 0:128]
                            Lp = lambda h2: cL0[j][:, h2, :]
                            bNp, bLp = b_cLM2[j], b_cL0[j]
                            Qp = None
                            bQp = b_c
                            nlev = 3 if sample else NLEV
                            for lev in range(1, nlev + 2):
                                last = (lev == nlev + 1)
                                pp = lev % 2
                                for h2 in range(2):
                                    if not last and lev < nlev:
                                        T(lambda: nc.tensor.matmul(PS[:, nlb, h2 * 128:(h2 + 1) * 128], Lp(h2), Np(h2), start=True, stop=True), [bLp, bNp], [nlbuf])
                                    if not last:
                                        T(lambda: nc.tensor.matmul(PS[:, nlb, 256 + h2 * 128:256 + (h2 + 1) * 128], Np(h2), Lp(h2), start=True, stop=True), [bLp, bNp], [nlbuf])
                                    qrhs = identb[:] if Qp is None else Qp(h2)
                                    T(lambda: nc.tensor.matmul(qap[:, h2 * 128:(h2 + 1) * 128], Lp(h2), qrhs, start=True, stop=True), [bLp, bQp], [qbuf])
                                if Qp is None:
                                    V(lambda: nc.vector.tensor_tensor(cQ[j][pp][:], qap.rearrange("p (h x) -> p h x", h=2),
                                                                      identb[:].unsqueeze(1).to_broadcast([128, 2, 128]), ALU.add), [qbuf, b_c], [b_cQ[j][pp]])
                                else:
                                    V(lambda: nc.vector.tensor_tensor(cQ[j][pp][:], qap.rearrange("p (h x) -> p h x", h=2), cQ[j][1 - pp][:], ALU.add),
                                      [qbuf, b_cQ[j][1 - pp]], [b_cQ[j][pp]])
                                if not last:
                                    A(lambda: nc.scalar.copy(cNL[j][pp][:], PS[:, nlb, :].rearrange("p (a h x) -> p a h x", a=2, h=2)), [nlbuf], [b_cNL[j][pp]])
                                    Np = (lambda pp_: (lambda h2: cNL[j][pp_][:, 0, h2, :]))(pp)
                                    Lp = (lambda pp_: (lambda h2: cNL[j][pp_][:, 1, h2, :]))(pp)
                                    bNp = bLp = b_cNL[j][pp]
                                Qp = (lambda pp_: (lambda h2: cQ[j][pp_][:, h2, :]))(pp)
                                bQp = b_cQ[j][pp]
                                yield
                            res[j] = (Qp, bQp)

                        def gen_state(cc, bs, j, res):
                            sl = slice(j * 128, (j + 1) * 128)
                            TT, bTT = res[j]
                            LM1, LM2, b_LM1, b_LM2 = cLM1[j], cLM2[j], b_cLM1[j], b_cLM2[j]
                            ARm, TM3, b_ARm, b_TM3 = oARm[bs], oTM3[bs], b_oARm[bs], b_oTM3[bs]
                            for h2 in range(2):
                                hs = slice(h2 * 64, (h2 + 1) * 64)
                                xo = PS[:, 0, h2 * 64:(h2 + 1) * 64]
                                if not sample:
                                    T(lambda: nc.tensor.matmul(xo, ARm[:, h2, j, 0, :], Hb[:, cc, :], start=True, stop=False), [b_ARm, b_H[cc]], [ph[0][0]])
                                else:
                                    for half in range(2):
                                        V(lambda: nc.vector.tensor_tensor(smask[:], ARm[:, h2, 0, 0, :].unsqueeze(1).to_broadcast([128, 8, 128]),
                                                                          colmask[:, half * 8:(half + 1) * 8, :], ALU.mult), [b_ARm, b_c], [b_smask])
                                        for q in range(8):
                                            s = half * 8 + q
                                            T(lambda: nc.tensor.matmul(xo, smask[:, q, :], Hsb[:, s, :], start=(s == 0), stop=False), [b_smask, b_Hs[cc]], [ph[0][0]])
                                T(lambda: nc.tensor.matmul(xo, LM1[:, h2, 0:128], TM3[:, j, hs], start=False, stop=True), [b_LM1, b_TM3], [ph[0][0]])
                            V(lambda: nc.vector.tensor_copy(Xs[:], PS[:, 0, 0:128].rearrange("p (h v) -> p h v", h=2)), [ph[0][0]], [b_Xs])
                            yield
                            for h2 in range(2):
                                T(lambda: nc.tensor.matmul(PS[:, 0, 128 + h2 * 64:128 + (h2 + 1) * 64], TT(h2), Xs[:, h2, :], start=True, stop=True), [bTT, b_Xs], [ph[0][0]])
                            V(lambda: nc.vector.tensor_copy(Ub[:], PS[:, 0, 128:256].rearrange("p (h v) -> p h v", h=2)), [ph[0][0]], [b_Ub])
                            yield
                            for h2 in range(2):
                                hs = slice(h2 * 64, (h2 + 1) * 64)
                                oo = PS[hs, 1, sl]
                                if not sample:
                                    T(lambda: nc.tensor.matmul(oo, Hb[:, cc, :], ARm[:, h2, j, 1, :], start=True, stop=False), [b_H[cc], b_ARm], [pb[1]])
                                else:
                                    for s in range(16):
                                        T(lambda: nc.tensor.matmul(PS[hs, 1, s * 8:(s + 1) * 8], Hsb[:, s, :], ARm[:, h2, 0, 1, s * 8:(s + 1) * 8],
                                                                   start=(s == 0), stop=False, skip_group_check=True), [b_Hs[cc], b_ARm], [pb[1]])
                                T(lambda: nc.tensor.matmul(oo, Ub[:, h2, :], LM2[:, h2, 128:256], start=False, stop=False, skip_group_check=sample), [b_Ub, b_LM2], [pb[1]])
                                T(lambda: nc.tensor.matmul(oo, TM3[:, j, hs], LM1[:, h2, 128:256], start=False, stop=True, skip_group_check=sample), [b_TM3, b_LM1], [pb[1]])
                                if not sample:
                                    ho = PS[hs, 0, 256:320]
                                    T(lambda: nc.tensor.matmul(ho, TM3[:, j, 128 + h2 * 64:128 + (h2 + 1) * 64], Ub[:, h2, :], start=True, stop=False), [b_TM3, b_Ub], [ph[0][1]])
                                    T(lambda: nc.tensor.matmul(ho, TM3[:, j, 256 + h2 * 64:256 + (h2 + 1) * 64], TM3[:, j, hs], start=False, stop=True), [b_TM3], [ph[0][1]])
                                else:
                                    sm4 = smask[:].rearrange("p s (a v) -> p a s v", a=2)
                                    for half in range(2):
                                        rmb = rowmask[:, half * 8:(half + 1) * 8].unsqueeze(2).to_broadcast([128, 8, 64])
                                        V(lambda: nc.vector.tensor_tensor(sm4[:, 0, :, :], Ub[:, h2, :].unsqueeze(1).to_broadcast([128, 8, 64]), rmb, ALU.mult),
                                          [b_Ub, b_c], [b_smask])
                                        V(lambda: nc.vector.tensor_tensor(sm4[:, 1, :, :], TM3[:, 0, hs].unsqueeze(1).to_broadcast([128, 8, 64]), rmb, ALU.mult),
                                          [b_TM3, b_c], [b_smask])
                                        for q in range(8):
                                            ho = PS[hs, 2 + half, q * 64:(q + 1) * 64]
                                            T(lambda: nc.tensor.matmul(ho, TM3[:, 0, 128 + h2 * 64:128 + (h2 + 1) * 64], sm4[:, 0, q, :], start=True, stop=False),
                                              [b_TM3, b_smask], [pb[2 + half]])
                                            T(lambda: nc.tensor.matmul(ho, TM3[:, 0, 256 + h2 * 64:256 + (h2 + 1) * 64], sm4[:, 1, q, :], start=False, stop=True),
                                              [b_TM3, b_smask], [pb[2 + half]])
                            if not sample:
                                V(lambda: nc.vector.scalar_tensor_tensor(Hf[:, cc, :], Hf[:, cc, :], oE1[bs][:, j * 128 + 127:j * 128 + 128], PS[:, 0, 256:320],
                                                                         ALU.mult, ALU.add), [b_H[cc], b_oE1[bs], ph[0][1]], [b_H[cc]])
                                A(lambda: nc.scalar.copy(Hb[:, cc, :], Hf[:, cc, :]), [b_H[cc]], [b_H[cc]])
                            else:
                                gcs = oE1[bs][:, 0:128].rearrange("p (q r) -> p q r", q=16)[:, :, 7:8].to_broadcast([128, 16, 64])
                                V(lambda: nc.vector.tensor_tensor(Hsf[:], Hsf[:], gcs, ALU.mult), [b_Hs[cc], b_oE1[bs]], [b_Hs[cc]])
                                for a_ in range(2):
                                    V(lambda: nc.vector.tensor_tensor(Hsf[:, a_ * 8:(a_ + 1) * 8, :], Hsf[:, a_ * 8:(a_ + 1) * 8, :],
                                                                      PS[:, 2 + a_, :].rearrange("p (q v) -> p q v", q=8), ALU.add),
                                      [b_Hs[cc], pb[2 + a_]], [b_Hs[cc]])
                                store_sample_state(cc)
                            yield

                        def gen_gn(cc, bs):
                            g1, g2, oT_ = (gnt[0][:, 0:ntok], gnt[1][:, 0:ntok], gnt[2][:, 0:ntok])
                            bg1, bg2, boT = b_gnt
                            A(lambda: nc.scalar.copy(oT_, PS[:, 1, 0:ntok]), [pb[1]], [boT])
                            V(lambda: nc.vector.tensor_tensor(g1, oT_, oT_, ALU.mult), [boT], [bg1])
                            pm_, pq_ = PS[:, 4, 0:ntok], PS[:, 4, 256:256 + ntok]
                            T(lambda: nc.tensor.matmul(pm_, bonesf[:], oT_, start=True, stop=True), [b_c, boT], [ph[4][0]])
                            T(lambda: nc.tensor.matmul(pq_, bonesf[:], g1, start=True, stop=True), [b_c, bg1], [ph[4][1]])
                            yield
                            A(lambda: nc.scalar.mul(g2, pm_, 1.0 / 64), [ph[4][0]], [bg2])
                            V(lambda: nc.vector.tensor_tensor(g1, g2, g2, ALU.mult), [bg2], [bg1])
                            V(lambda: nc.vector.scalar_tensor_tensor(g1, pq_, 1.0 / 64, g1, ALU.mult, ALU.subtract), [ph[4][1], bg1], [bg1])
                            A(lambda: nc.scalar.activation(g1, g1, AF.Ln, bias=epsc[:, 1:2]), [bg1, b_c], [bg1])
                            A(lambda: nc.scalar.activation(g1, g1, AF.Exp, scale=-0.5), [bg1], [bg1])
                            yield
                            V(lambda: nc.vector.tensor_tensor(oT_, oT_, g2, ALU.subtract), [boT, bg2], [boT])
                            V(lambda: nc.vector.tensor_tensor(oT_, oT_, g1, ALU.mult), [boT, bg1], [boT])
                            A(lambda: nc.scalar.activation(oT_, oT_, AF.Identity, scale=pv[:, cc, LXG:LXG + 1], bias=pv[:, cc, LXB:LXB + 1]),
                              [boT, b_pv], [boT])
                            yield
                            V(lambda: nc.vector.tensor_tensor(oT_, oT_, obon[bs][:, 0:ntok], ALU.add), [boT, b_obon[bs]], [boT])
                            V(lambda: nc.vector.tensor_tensor(bT[:, cc, 0:ntok], oT_, ogt[bs][:, 0:ntok], ALU.mult), [boT, b_ogt[bs]], [b_bT])
                            yield

                        def gen_units(cc, bs):
                            res = {}
                            for j in range(nt):
                                yield from gen_mats(cc, bs, j)
                            if nt == 2:
                                yield from merge(gen_chain(cc, bs, 0, res), gen_chain(cc, bs, 1, res))
                            else:
                                yield from gen_chain(cc, bs, 0, res)
                            for j in range(nt):
                                yield from gen_state(cc, bs, j, res)
                            yield from gen_gn(cc, bs)

                        def gen_body():
                            if sample:
                                for cc in range(4):
                                    yield from gen_prep(cc, 0)
                                    yield from gen_units(cc, 0)
                            else:
                                for cc in range(4):
                                    u = gen_units(cc, cc % 2)
                                    if cc < 3:
                                        yield from merge(u, gen_prep(cc + 1, (cc + 1) % 2))
                                    else:
                                        yield from u

                        def gen_post():
                            for j, ti in enumerate(tiles):
                                mb = 2 + 2 * (j % 2)
                                for hf in range(2):
                                    for kc in range(8):
                                        lhsT = aT_all[:, kc, ti * 128:(ti + 1) * 128] if kc < 4 else bT[:, kc - 4, j * 128:(j + 1) * 128]
                                        T(lambda: nc.tensor.matmul(PS[:, mb + hf, :], lhsT, woutb[:, kc, hf * 512:(hf + 1) * 512], start=(kc == 0), stop=(kc == 7)),
                                          [b_aT, b_bT, b_wB], [pb[mb + hf]])
                                    yield
                                yield from layer_norm_tile(ti, PS[:, mb:mb + 2, :].rearrange("p a b -> p (a b)"), [pb[mb], pb[mb + 1]], gen=True)

                        def gen_post1(bank=3):
                            for j, ti in enumerate(tiles):
                                for hf in range(2):
                                    for kc in range(8):
                                        lhsT = aT_all[:, kc, ti * 128:(ti + 1) * 128] if kc < 4 else bT[:, kc - 4, j * 128:(j + 1) * 128]
                                        T(lambda: nc.tensor.matmul(PS[:, bank, :], lhsT, woutb[:, kc, hf * 512:(hf + 1) * 512], start=(kc == 0), stop=(kc == 7)),
                                          [b_aT, b_bT, b_wB], [pb[bank]])
                                    V(lambda: nc.vector.scalar_tensor_tensor(lntmp[:, hf * 512:(hf + 1) * 512], x_all[:, ti, hf * 512:(hf + 1) * 512], ALPHA,
                                                                             PS[:, bank, :], ALU.mult, ALU.add), [bx[ti], pb[bank]], [b_lntmp])
                                    yield
                                yield from layer_norm_tile(ti, None, [], gen=True)

                        return gen_pre, gen_body, gen_post, gen_units, gen_prep, gen_post1

                    fns = [group_fns(gi, tiles, ntok) for gi, (tiles, ntok) in enumerate(GROUPS_B)]
                    run(fns[0][0]())
                    run(fns[0][1]())
                    run(fns[0][2]())
                    sample_to_prompt_transition()
                    run(fns[1][0]())
                    NG = len(GROUPS_B)
                    for gi in range(1, NG):
                        _, _, _, g_units, g_prep, _ = fns[gi]
                        for cc in range(4):
                            streams = [g_units(cc, cc % 2)]
                            if cc < 3:
                                streams.append(g_prep(cc + 1, (cc + 1) % 2))
                            elif gi + 1 < NG:
                                streams.append(fns[gi + 1][0]())
                            if cc == 0 and gi > 1:
                                streams.append(fns[gi - 1][5]())
                            run(merge_all(streams))
                    run(fns[NG - 1][2]())
                    wkvo = lntmp[0:64, 0:512].rearrange("p (c x) -> p c x", c=4)
                    b_wkvo = Multi(b_lnq[0:2])
                    b_sho1 = Multi(b_lnq[2:4])
                    for q in range(4):
                        n_ = 4 if q < 3 else 2
                        for r in range(n_):
                            ci = q * 4 + r
                            T(lambda: nc.tensor.transpose(PS[0:1, 4 + q, r * 128:(r + 1) * 128], carry[:, ci:ci + 1], ident[:]), [b_carry[ci], b_c], [pb[4 + q]])
                        V(lambda: nc.vector.tensor_copy(lntmp[0:1, 512:512 + n_ * 128], PS[0:1, 4 + q, 0:n_ * 128]), [pb[4 + q]], [b_sho1])
                        odma(o_shift_p[:, q * 512:q * 512 + n_ * 128], lntmp[0:1, 512:512 + n_ * 128], [b_sho1])
                    for cc in range(4):
                        T(lambda: nc.tensor.transpose(PS[0:64, 0, cc * 128:(cc + 1) * 128], Hf[:, cc, :], ident[:]), [b_H[cc], b_c], [pb[0]])
                    V(lambda: nc.vector.tensor_copy(wkvo[:], PS[0:64, 0, :].rearrange("p (c x) -> p c x", c=4)), [pb[0]], [b_wkvo])
                    for cc in range(4):
                        for h2 in range(2):
                            odma(o_wkv_p[2 * cc + h2, :, :], wkvo[:, cc, h2 * 64:(h2 + 1) * 64], [b_wkvo])
                    S.barrier()
                S.barrier()

            def ffn(layer, final):
                with ExitStack() as fs:
                    load_ln(ln_ffn_g, ln_ffn_b, layer)
                    xT = sbt(fs, "xT_f", (128, 8, 640), BF16)
                    hT = sbt(fs, "hT", (128, 22, 640), BF16)
                    wd = sbt(fs, "wd", (128, 22, 1024), BF16)
                    NSL = 3
                    wg = [sbt(fs, "wg%d" % i, (128, 8, 256), BF16) for i in range(NSL)]
                    wu = [sbt(fs, "wu%d" % i, (128, 8, 256), BF16) for i in range(NSL)]
                    sg = [sbt(fs, "sg%d" % i, (128, 512), F32) for i in range(2)]
                    b_xT, b_hT, b_wd = Buf("xTf"), [Buf("hT%d" % i) for i in range(22)], Buf("wd")
                    b_wg = [Buf("wg%d" % i) for i in range(NSL)]
                    slab_state = {"next": 0}

                    def slab_prefetch(upto):
                        while slab_state["next"] <= upto and slab_state["next"] < 44:
                            n_ = slab_state["next"]
                            sl_ = n_ % 11
                            S.dma("pool", wg[n_ % NSL][:], ffn_gate[layer, :, sl_ * 256:(sl_ + 1) * 256].rearrange("(k p) n -> p k n", p=128), writes=[b_wg[n_ % NSL]])
                            S.dma("pool", wu[n_ % NSL][:], ffn_up[layer, :, sl_ * 256:(sl_ + 1) * 256].rearrange("(k p) n -> p k n", p=128), writes=[b_wg[n_ % NSL]], parallel=True)
                            slab_state["next"] = n_ + 1
                    b_sg = [Buf("sg0"), Buf("sg1")]
                    PASSES = ([0, 1, 2, 3], [4, 5, 6, 7], [8, 9, 10, 11], [12, 13, 14, 15, 16])
                    for pi, tiles in enumerate(PASSES):
                        ntk = len(tiles) * 128
                        if pi == 0:
                            slab_prefetch(NSL - 1)
                            make_xT(xT, b_xT, tiles)
                        grp = [(o, min(512, ntk - o)) for o in range(0, ntk, 512)]
                        k = 0
                        for sl in range(11):
                            use_ = pi * 11 + sl
                            wb_ = use_ % NSL
                            slab_prefetch(use_ + NSL - 1)
                            if pi == 0:
                                S.dma("pool", wd[:, 2 * sl:2 * sl + 2, :], ffn_down[layer, sl * 256:(sl + 1) * 256, :].rearrange("(f p) d -> p f d", p=128), writes=[b_wd], parallel=True)
                            for f2 in range(2):
                                fc = sl * 2 + f2
                                for (o, n) in grp:
                                    bg, bu = 2 * (k % 2), 2 * (k % 2) + 1
                                    for kc in range(8):
                                        T(lambda: nc.tensor.matmul(PS[:, bg, 0:n], wg[wb_][:, kc, f2 * 128:(f2 + 1) * 128], xT[:, kc, o:o + n], start=(kc == 0), stop=(kc == 7)),
                                          [b_wg[wb_], b_xT], [pb[bg]])
                                    for kc in range(8):
                                        T(lambda: nc.tensor.matmul(PS[:, bu, 0:n], wu[wb_][:, kc, f2 * 128:(f2 + 1) * 128], xT[:, kc, o:o + n], start=(kc == 0), stop=(kc == 7)),
                                          [b_wg[wb_], b_xT], [pb[bu]])
                                    A(lambda: nc.scalar.activation(sg[k % 2][:, 0:n], PS[:, bg, 0:n], AF.Silu), [pb[bg]], [b_sg[k % 2]])
                                    V(lambda: nc.vector.tensor_tensor(hT[:, fc, o:o + n], sg[k % 2][:, 0:n], PS[:, bu, 0:n], ALU.mult), [b_sg[k % 2], pb[bu]], [b_hT[fc]])
                                    k += 1
                        if pi + 1 < len(PASSES):
                            make_xT(xT, b_xT, PASSES[pi + 1])
                        for j, ti in enumerate(tiles):
                            b0 = 4 + 2 * (j % 2)
                            for hf in range(2):
                                for fc in range(22):
                                    T(lambda: nc.tensor.matmul(PS[:, b0 + hf, :], hT[:, fc, j * 128:(j + 1) * 128], wd[:, fc, hf * 512:(hf + 1) * 512],
                                                               start=(fc == 0), stop=(fc == 21)), [b_hT[fc], b_wd], [pb[b0 + hf]])
                            dst = None
                            if final:
                                dst = y_s if ti == 16 else y_p[ti * 128:(ti + 1) * 128, :]
                            layer_norm_tile(ti, PS[:, b0:b0 + 2, :].rearrange("p a b -> p (a b)"), [pb[b0], pb[b0 + 1]], final_dst=dst)
                    S.barrier()

            checkpoint(4)
            if not os.environ.get("MK_SKIP_FFN0"):
                ffn(0, False)
            checkpoint(5)

            with ExitStack() as l1:
                load_ln(ln_mix_g, ln_mix_b, 1)
                band = sbt(l1, "band", (128, 4, 6, 128), F32)
                pw = sbt(l1, "pw", (128, 4, 2, 256), BF16)
                psc = sbt(l1, "psc", (128, D), F32)
                spl = sbt(l1, "spl", (128, 2, D), F32)
                pT = sbt(l1, "pT", (128, 8, 128), BF16)
                ysc = sbt(l1, "ysc", (128, D), F32)
                b_pc, b_spl, b_pT, b_ysc = Buf("pc"), Buf("spl"), Buf("pT"), Buf("ysc")
                S.dma("sp", band[:], cd["c_band"], writes=[b_pc])
                for g in range(4):
                    S.dma("pool", pw[:, g, :, :], pool_w[g].rearrange("(k p) d -> p k d", p=128), writes=[b_pc])
                S.dma("sp", psc[:], pool_scale[0].partition_broadcast(128), writes=[b_pc])
                S.dma("sp", spl[0:120, :, :], spool.rearrange("(h q) r d -> (q r) h d", h=2), writes=[b_spl])
                odma(o_pool_p, x_all[113:128, 15, :], [bx[15]])
                odma(o_pool_s[:, 7:15, :], x_all[:, 16, :], [bx[16]])
                odma(o_pool_s[:, 0:7, :], spool[:, 8:15, :], [])
                for ti in [16] + list(range(15, -1, -1)):
                    for ci in range(8):
                        g = ci // 2
                        cs = slice(ci * 128, (ci + 1) * 128)
                        po = PS[:, ci // 4, (ci % 4) * 128:(ci % 4 + 1) * 128]
                        bk = pb[ci // 4]
                        if ti == 16:
                            T(lambda: nc.tensor.matmul(po, spl[0:120, 0, cs], band[0:120, g, 4, :], start=True, stop=False), [b_spl, b_pc], [bk])
                            T(lambda: nc.tensor.matmul(po, spl[0:120, 1, cs], band[0:120, g, 5, :], start=False, stop=False), [b_spl, b_pc], [bk])
                            T(lambda: nc.tensor.matmul(po, x_all[:, 16, cs], band[:, g, 3, :], start=False, stop=True), [bx[16], b_pc], [bk])
                        elif ti == 0:
                            T(lambda: nc.tensor.matmul(po, x_all[:, 0, cs], band[:, g, 2, :], start=True, stop=True), [bx[0], b_pc], [bk])
                        else:
                            T(lambda: nc.tensor.matmul(po, x_all[:, ti - 1, cs], band[:, g, 1, :], start=True, stop=False), [bx[ti - 1], b_pc], [bk])
                            T(lambda: nc.tensor.matmul(po, x_all[:, ti, cs], band[:, g, 0, :], start=False, stop=True), [bx[ti], b_pc], [bk])
                    A(lambda: nc.scalar.copy(pT[:, 0:4, :], PS[:, 0, :].rearrange("p (c t) -> p c t", c=4)), [pb[0]], [b_pT])
                    V(lambda: nc.vector.tensor_copy(pT[:, 4:8, :], PS[:, 1, :].rearrange("p (c t) -> p c t", c=4)), [pb[1]], [b_pT])
                    for g in range(4):
                        for k2 in range(2):
                            T(lambda: nc.tensor.matmul(PS[:, 2 + g // 2, (g % 2) * 256:(g % 2 + 1) * 256], pT[:, 2 * g + k2, :], pw[:, g, k2, :],
                                                       start=(k2 == 0), stop=(k2 == 1)), [b_pT, b_pc], [pb[2 + g // 2]])
                    V(lambda: nc.vector.tensor_tensor(ysc[:], PS[:, 2:4, :].rearrange("p a b -> p (a b)"), psc[:], ALU.mult), [pb[2], pb[3], b_pc], [b_ysc])
                    layer_norm_tile(ti, ysc[:], [b_ysc])
                S.barrier()

            checkpoint(6)
            ffn(1, True)


        except StopBuild:
            stopped.append(1)

        for b in outbufs:
            if b.w is not None:
                S._wait("sp", b.w[0], b.w[1])
        S.barrier()
        print("instructions", S.nins, "waits", S.nwait)
        if stopped:
            ctx.pop_all()
    return nc, consts


_CACHE = {}


def kernel(**inp):
    f = lambda a: np.ascontiguousarray(np.asarray(a, dtype=np.float32))
    if "nc" not in _CACHE:
        _CACHE["nc"] = build()
    nc, consts = _CACHE["nc"]
    vec_names = ["conv_b", "conv_ln_g", "conv_ln_b", "rwkv_w0", "rwkv_a0", "rwkv_kk", "rwkv_ka", "rwkv_rk", "rwkv_lnx_g", "rwkv_lnx_b"]
    vecs = np.stack([f(inp[n])[0].reshape(512) for n in vec_names], axis=0)
    shared = {
        "w_in": f(inp["w_in"])[0], "conv_w": f(inp["conv_w"])[0], "vecs": f(vecs),
        "mu": f(inp["rwkv_mu"])[0].reshape(14, 128),
        "w2a2": f(np.concatenate([f(inp["rwkv_w2"])[0], f(inp["rwkv_a2"])[0]], axis=0)),
        "g2": f(inp["rwkv_g2"])[0], "w_out": f(inp["w_out"])[0], "pool_w": f(inp["pool_w"])[0],
        "pool_scale": f(inp["pool_scale"]), "ln_mix_g": f(inp["ln_mix_g"]), "ln_mix_b": f(inp["ln_mix_b"]),
        "ffn_gate": f(inp["ffn_gate"]), "ffn_up": f(inp["ffn_up"]), "ffn_down": f(inp["ffn_down"]),
        "ln_ffn_g": f(inp["ln_ffn_g"]), "ln_ffn_b": f(inp["ln_ffn_b"]),
    }
    shared.update(consts)
    xp, xs = f(inp["x_prompt"]), f(inp["x_sample"])
    sc, ss, sw, sp_ = f(inp["state_conv"])[0], f(inp["state_shift"])[0], f(inp["state_wkv"])[0], f(inp["state_pool"])[0]
    in_maps = []
    for c in range(NCORES):
        m = dict(shared)
        q = slice(16 * c, 16 * c + 16)
        m["xp"] = xp[c]
        m["xs"] = xs[q].reshape(128, D)
        m["sconv"] = sc[q]
        m["sshift"] = ss[q]
        m["swkv"] = sw[q]
        m["spool"] = sp_[q]
        in_maps.append(m)
    res = run_bass_kernel_spmd(nc, in_maps, core_ids=list(range(NCORES)))
    R = res.results
    cat = lambda k: np.concatenate([np.asarray(R[c][k], dtype=np.float32) for c in range(NCORES)], axis=0)
    y_prompt = np.stack([np.asarray(R[c]["y_p"], np.float32) for c in range(NCORES)], axis=0)
    y_sample = cat("y_s").reshape(128, 8, D)
    conv_p = np.stack([R[c]["o_conv_p"] for c in range(NCORES)], 0)[None].astype(np.float32)
    shift_p = cat("o_shift_p")[None]
    wkv_p = np.stack([R[c]["o_wkv_p"] for c in range(NCORES)], 0)[None].astype(np.float32)
    pool_p = np.stack([R[c]["o_pool_p"] for c in range(NCORES)], 0)[None].astype(np.float32)
    conv_s = cat("o_conv_s")[None]
    shift_s = cat("o_shift_s")[None]
    wkv_s = cat("o_wkv_s")[None]
    pool_s = cat("o_pool_s")[None]
    return (y_prompt, y_sample, conv_p, shift_p, wkv_p, pool_p, conv_s, shift_s, wkv_s, pool_s)
```
